# Optimizing a Trainium2 kernel written in Bass

```python
import math
import jax, jax.numpy as jnp
from jax import lax
import numpy as np

D_MODEL = 2048
BATCH = 4
SEQ = 4096
DEPTH = 4

N_META = 16
D_FF = 5632
SSM_WIDTH = D_MODEL // 2
SSM_GROUP = 16
SSM_GROUPS = SSM_WIDTH // SSM_GROUP
SSM_STATE = 64
DT_MIN = 1e-3
DT_MAX = 1e-1
RWKV_WIDTH = D_MODEL // 2
RWKV_HEAD = 64
RWKV_HEADS = RWKV_WIDTH // RWKV_HEAD
LORA_DECAY = 64
LORA_ICLR = 64
LORA_VRES = 32
LORA_GATE = 160
NORM_EPS = 1e-6
LNX_EPS = RWKV_HEAD * 1e-5

COL_U = 0
COL_R = COL_U + SSM_WIDTH
COL_K = COL_R + RWKV_WIDTH
COL_V = COL_K + RWKV_WIDTH
COL_W = COL_V + RWKV_WIDTH
COL_A = COL_W + LORA_DECAY
COL_G = COL_A + LORA_ICLR
COL_GATE_A = COL_G + LORA_GATE
COL_GATE_B = COL_GATE_A + D_MODEL
P_COMMON = COL_GATE_B + D_MODEL
P_REST = P_COMMON + LORA_VRES
SHIFT_LO = COL_R
SHIFT_HI = COL_GATE_A

kernel_name = "hybrid_s5_rwkv7_macaron_meta"


def rms_norm(x, g):
    xf = x.astype(jnp.float32)
    inv = lax.rsqrt(jnp.mean(xf * xf, axis=-1, keepdims=True) + NORM_EPS)
    return (xf * inv).astype(x.dtype) * g


def swiglu(h, w_gate, w_up, w_down):
    return (jax.nn.silu(h @ w_gate) * (h @ w_up)) @ w_down


def token_shift(p):
    return jnp.pad(p, ((0, 0), (1, 0), (0, 0)))[:, :-1]


def cmul(ar, ai, br, bi):
    return ar * br - ai * bi, ar * bi + ai * br


def s5_branch(u, lam_re, lam_im, log_dt, b_re, b_im, c_re, c_im, d_skip, w_glu):
    f32 = jnp.float32
    bsz, t_len, _ = u.shape
    uf = u.astype(f32).reshape(bsz, t_len, SSM_GROUPS, SSM_GROUP)
    lr = lam_re.astype(f32)
    li = lam_im.astype(f32)
    dt = jnp.exp(log_dt.astype(f32))[:, None]
    mag = jnp.exp(lr * dt)
    abar_re = mag * jnp.cos(li * dt)
    abar_im = mag * jnp.sin(li * dt)
    den = lr * lr + li * li
    nr = abar_re - 1.0
    ni = abar_im
    q_re = (nr * lr + ni * li) / den
    q_im = (ni * lr - nr * li) / den
    bb_re, bb_im = cmul(q_re[..., None], q_im[..., None], b_re.astype(f32), b_im.astype(f32))
    drive_re = jnp.einsum("btgc,gnc->btgn", uf, bb_re)
    drive_im = jnp.einsum("btgc,gnc->btgn", uf, bb_im)
    a_re = jnp.broadcast_to(abar_re, (1, t_len, SSM_GROUPS, SSM_STATE))
    a_im = jnp.broadcast_to(abar_im, (1, t_len, SSM_GROUPS, SSM_STATE))

    def combine(e1, e2):
        a1r, a1i, b1r, b1i = e1
        a2r, a2i, b2r, b2i = e2
        ar, ai = cmul(a2r, a2i, a1r, a1i)
        br, bi = cmul(a2r, a2i, b1r, b1i)
        return ar, ai, br + b2r, bi + b2i

    _, _, s_re, s_im = lax.associative_scan(combine, (a_re, a_im, drive_re, drive_im), axis=1)
    y = (jnp.einsum("btgn,gcn->btgc", s_re, c_re.astype(f32))
         - jnp.einsum("btgn,gcn->btgc", s_im, c_im.astype(f32)))
    y = y.reshape(bsz, t_len, SSM_WIDTH) + d_skip.astype(f32) * uf.reshape(bsz, t_len, SSM_WIDTH)
    y = jax.nn.gelu(y).astype(u.dtype)
    return y * jax.nn.sigmoid(y @ w_glu)


def _heads(t):
    return t.reshape(t.shape[0], t.shape[1], RWKV_HEADS, RWKV_HEAD)


def rwkv7_recurrence(r, w, k, v, a, b):
    bsz = r.shape[0]

    def step(s, inp):
        r_t, w_t, k_t, v_t, a_t, b_t = inp
        sa = jnp.einsum("bhvk,bhk->bhv", s, a_t)
        s = s * w_t[:, :, None, :] + sa[..., None] * b_t[:, :, None, :] + v_t[..., None] * k_t[:, :, None, :]
        return s, jnp.einsum("bhvk,bhk->bhv", s, r_t)

    s0 = jnp.zeros((bsz, RWKV_HEADS, RWKV_HEAD, RWKV_HEAD), jnp.float32)
    xs = tuple(jnp.moveaxis(t, 1, 0) for t in (r, w, k, v, a, b))
    _, out = lax.scan(step, s0, xs)
    return jnp.moveaxis(out, 0, 1)


def rwkv7_branch(r, k, v, xw, xa, xg, w0, w2, a0, a2, g2, k_k, k_a, r_k, lnx_w, lnx_b):
    f32 = jnp.float32
    bsz, t_len, _ = r.shape
    w_log = -jax.nn.softplus(-(w0 + jnp.tanh(xw) @ w2)) - 0.5
    decay = jnp.exp(-jnp.exp(w_log.astype(f32)))
    iclr = jax.nn.sigmoid(a0 + xa @ a2)
    gate = jax.nn.sigmoid(xg) @ g2
    kk = _heads(k * k_k).astype(f32)
    kk = kk * lax.rsqrt(jnp.sum(kk * kk, axis=-1, keepdims=True) + 1e-12)
    k = k * (1.0 + (iclr - 1.0) * k_a)
    r_h = _heads(r).astype(f32)
    k_h = _heads(k).astype(f32)
    v_h = _heads(v).astype(f32)
    iclr_h = _heads(iclr).astype(f32)
    o = rwkv7_recurrence(r_h, _heads(decay), k_h, v_h, -kk, kk * iclr_h)
    mean = jnp.mean(o, axis=-1, keepdims=True)
    var = jnp.mean(jnp.square(o - mean), axis=-1, keepdims=True)
    o = (o - mean) * lax.rsqrt(var + LNX_EPS)
    bonus = jnp.sum(r_h * k_h * r_k.astype(f32), axis=-1, keepdims=True) * v_h
    o = (o.reshape(bsz, t_len, RWKV_WIDTH).astype(r.dtype) * lnx_w + lnx_b
         + bonus.reshape(bsz, t_len, RWKV_WIDTH).astype(r.dtype))
    return o * gate


def setup_inputs(seed: int = 0) -> dict:
    key = jax.random.key(seed)
    ks = iter(jax.random.split(key, 64))
    f32 = jnp.float32
    L = DEPTH
    Lr = DEPTH - 1
    G, N, C = SSM_GROUPS, SSM_STATE, SSM_GROUP

    def nrm(shape, scale):
        return scale * jax.random.normal(next(ks), shape, f32)

    def gain(shape):
        return 1.0 + nrm(shape, 0.02)

    def unif(shape, lo, hi):
        return jax.random.uniform(next(ks), shape, f32, lo, hi)

    n_idx = jnp.arange(SSM_STATE, dtype=f32)
    d_in = D_MODEL ** -0.5
    return {
        "x": nrm((BATCH, SEQ, D_MODEL), 1.0),
        "meta_tokens": nrm((N_META, D_MODEL), 1.0),
        "ffn1_norm": gain((L, D_MODEL)),
        "ffn1_w_gate": nrm((L, D_MODEL, D_FF), d_in),
        "ffn1_w_up": nrm((L, D_MODEL, D_FF), d_in),
        "ffn1_w_down": nrm((L, D_FF, D_MODEL), D_FF ** -0.5),
        "mix_norm": gain((L, D_MODEL)),
        "w_in_first": nrm((D_MODEL, P_COMMON), d_in),
        "w_in_rest": nrm((Lr, D_MODEL, P_REST), d_in),
        "mu_shift": unif((L, SHIFT_HI - SHIFT_LO), 0.0, 1.0),
        "mu_vres": unif((Lr, LORA_VRES), 0.0, 1.0),
        "ssm_lambda_re": -0.5 + nrm((L, G, N), 0.01),
        "ssm_lambda_im": math.pi * n_idx + nrm((L, G, N), 0.01),
        "ssm_log_dt": unif((L, G), math.log(DT_MIN), math.log(DT_MAX)),
        "ssm_b_re": nrm((L, G, N, C), (2.0 * C) ** -0.5),
        "ssm_b_im": nrm((L, G, N, C), (2.0 * C) ** -0.5),
        "ssm_c_re": nrm((L, G, C, N), N ** -0.5),
        "ssm_c_im": nrm((L, G, C, N), N ** -0.5),
        "ssm_d": nrm((L, SSM_WIDTH), 0.5),
        "ssm_w_glu": nrm((L, SSM_WIDTH, SSM_WIDTH), SSM_WIDTH ** -0.5),
        "rwkv_w0": unif((L, RWKV_WIDTH), -6.0, -1.0),
        "rwkv_w2": nrm((L, LORA_DECAY, RWKV_WIDTH), 0.5 * LORA_DECAY ** -0.5),
        "rwkv_a0": nrm((L, RWKV_WIDTH), 0.1),
        "rwkv_a2": nrm((L, LORA_ICLR, RWKV_WIDTH), 0.5 * LORA_ICLR ** -0.5),
        "rwkv_v0": nrm((Lr, RWKV_WIDTH), 0.1),
        "rwkv_v2": nrm((Lr, LORA_VRES, RWKV_WIDTH), 0.5 * LORA_VRES ** -0.5),
        "rwkv_g2": nrm((L, LORA_GATE, RWKV_WIDTH), LORA_GATE ** -0.5),
        "rwkv_k_k": 0.85 + nrm((L, RWKV_WIDTH), 0.02),
        "rwkv_k_a": gain((L, RWKV_WIDTH)),
        "rwkv_r_k": nrm((L, RWKV_HEADS, RWKV_HEAD), 0.1),
        "rwkv_lnx_w": gain((L, RWKV_WIDTH)),
        "rwkv_lnx_b": nrm((L, RWKV_WIDTH), 0.01),
        "w_up_ssm": nrm((L, SSM_WIDTH, D_MODEL), SSM_WIDTH ** -0.5),
        "w_up_rwkv": nrm((L, RWKV_WIDTH, D_MODEL), RWKV_WIDTH ** -0.5),
        "w_out": nrm((L, D_MODEL, D_MODEL), d_in),
        "ffn2_norm": gain((L, D_MODEL)),
        "ffn2_w_gate": nrm((L, D_MODEL, D_FF), d_in),
        "ffn2_w_up": nrm((L, D_MODEL, D_FF), d_in),
        "ffn2_w_down": nrm((L, D_FF, D_MODEL), D_FF ** -0.5),
        "final_norm": gain((D_MODEL,)),
    }


def reference(x, meta_tokens, ffn1_norm, ffn1_w_gate, ffn1_w_up, ffn1_w_down, mix_norm,
              w_in_first, w_in_rest, mu_shift, mu_vres,
              ssm_lambda_re, ssm_lambda_im, ssm_log_dt, ssm_b_re, ssm_b_im, ssm_c_re, ssm_c_im,
              ssm_d, ssm_w_glu,
              rwkv_w0, rwkv_w2, rwkv_a0, rwkv_a2, rwkv_v0, rwkv_v2, rwkv_g2,
              rwkv_k_k, rwkv_k_a, rwkv_r_k, rwkv_lnx_w, rwkv_lnx_b,
              w_up_ssm, w_up_rwkv, w_out,
              ffn2_norm, ffn2_w_gate, ffn2_w_up, ffn2_w_down, final_norm):
    bsz = x.shape[0]
    meta = jnp.broadcast_to(meta_tokens[None].astype(x.dtype), (bsz, N_META, D_MODEL))
    h_res = jnp.concatenate([meta, x], axis=1)
    v_first = None
    for i in range(DEPTH):
        h = rms_norm(h_res, ffn1_norm[i])
        h_res = h_res + 0.5 * swiglu(h, ffn1_w_gate[i], ffn1_w_up[i], ffn1_w_down[i])

        h = rms_norm(h_res, mix_norm[i])
        p = h @ (w_in_first if i == 0 else w_in_rest[i - 1])
        u = p[..., COL_U:COL_R]
        p_rw = p[..., SHIFT_LO:SHIFT_HI]
        p_rw = p_rw + mu_shift[i] * (token_shift(p_rw) - p_rw)
        xr = p_rw[..., COL_R - SHIFT_LO:COL_K - SHIFT_LO]
        xk = p_rw[..., COL_K - SHIFT_LO:COL_V - SHIFT_LO]
        xv = p_rw[..., COL_V - SHIFT_LO:COL_W - SHIFT_LO]
        xw = p_rw[..., COL_W - SHIFT_LO:COL_A - SHIFT_LO]
        xa = p_rw[..., COL_A - SHIFT_LO:COL_G - SHIFT_LO]
        xg = p_rw[..., COL_G - SHIFT_LO:COL_GATE_A - SHIFT_LO]
        if i == 0:
            v_first = xv
            v = xv
        else:
            xvr = p[..., P_COMMON:P_REST]
            xvr = xvr + mu_vres[i - 1] * (token_shift(xvr) - xvr)
            v = xv + (v_first - xv) * jax.nn.sigmoid(rwkv_v0[i - 1] + xvr @ rwkv_v2[i - 1])

        y_a = s5_branch(u, ssm_lambda_re[i], ssm_lambda_im[i], ssm_log_dt[i], ssm_b_re[i], ssm_b_im[i],
                        ssm_c_re[i], ssm_c_im[i], ssm_d[i], ssm_w_glu[i])
        y_b = rwkv7_branch(xr, xk, v, xw, xa, xg, rwkv_w0[i], rwkv_w2[i], rwkv_a0[i], rwkv_a2[i],
                           rwkv_g2[i], rwkv_k_k[i], rwkv_k_a[i], rwkv_r_k[i], rwkv_lnx_w[i], rwkv_lnx_b[i])
        g_a = jax.nn.sigmoid(p[..., COL_GATE_A:COL_GATE_B])
        g_b = jax.nn.sigmoid(p[..., COL_GATE_B:P_COMMON])
        merged = g_a * (y_a @ w_up_ssm[i]) + g_b * (y_b @ w_up_rwkv[i])
        h_res = h_res + merged @ w_out[i]

        h = rms_norm(h_res, ffn2_norm[i])
        h_res = h_res + 0.5 * swiglu(h, ffn2_w_gate[i], ffn2_w_up[i], ffn2_w_down[i])
    out = rms_norm(h_res, final_norm)
    return out[:, N_META:]
```

```python
import contextlib
import math
import numpy as np
import ml_dtypes
import concourse.bass as bass
import concourse.mybir as mybir
from concourse.bass_utils import run_bass_kernel_spmd

F32 = mybir.dt.float32
BF16 = mybir.dt.bfloat16
ALU = mybir.AluOpType
AF = mybir.ActivationFunctionType

D = 2048
DFF = 5632
NFF = DFF // 128
NDC = D // 128
SEQ = 4096
NMETA = 16
TPAD = 4160
PIN = 8512
C0 = math.exp(-0.5)
NORM_EPS = 1e-6
LNX_EPS = 64 * 1e-5
GELU_K = 2.0 * math.sqrt(2.0 / math.pi)
DEBUG_TENSOR = None
MIX_STOP = 0

COL_U, COL_R, COL_K, COL_V, COL_WA, COL_G, COL_GA, COL_GB, COL_VR = 0, 1024, 2048, 3072, 4096, 4224, 4384, 6432, 8480

PP = {}
_o = 0
for _n, _w in (("g1", 16), ("gm", 16), ("g2", 16), ("gf", 16), ("mur", 8), ("muk", 8), ("muv", 8),
               ("muwa", 1), ("mug1", 1), ("mug2", 1), ("muvr", 1), ("w0", 8), ("a0", 8), ("v0", 8),
               ("kk", 8), ("ka", 8), ("rk", 8), ("lw", 8), ("lb", 8), ("sd", 8),
               ("lre", 32), ("lim", 32), ("ldt", 32), ("flag", 1)):
    PP[_n] = (_o, _w)
    _o += _w
NPP = _o

CS = {}
_o = 0
for _n, _w in (("ones", 128), ("bones", 128), ("ident", 128), ("m1", 512), ("msl", 128), ("ifree", 128),
               ("rmask", 512)):
    CS[_n] = (_o, _w)
    _o += _w
NCST = _o


class Res:
    __slots__ = ("name", "w", "r", "dsem", "dcnt")

    def __init__(self, name):
        self.name = name
        self.w = None
        self.r = []
        self.dsem = None
        self.dcnt = 0


class Prog:
    ENG = ("pe", "dve", "act", "pool", "sp")

    def __init__(self, nc):
        self.nc = nc
        self.ops = {e: [] for e in self.ENG}
        self.cnt = {e: 0 for e in self.ENG}
        self.seen = {e: {} for e in self.ENG}
        self.semkeys = list(self.ENG)
        self.dma_res = []
        self.n_instr = 0
        self._rid = 0

    def res(self, name="r"):
        self._rid += 1
        return Res("%s%d" % (name, self._rid))

    def pres(self, key):
        if not hasattr(self, "_pres"):
            self._pres = {}
        if key not in self._pres:
            self._pres[key] = self.res(key)
        return self._pres[key]

    def _deps(self, eng, reads, writes, noself):
        ev = {}

        def add(e):
            if e is None:
                return
            k, v = e
            if noself and k == eng:
                return
            if ev.get(k, 0) < v:
                ev[k] = v
        for r in reads:
            add(r.w)
        for w in writes:
            add(w.w)
            for e in w.r:
                add(e)
        waits = []
        seen = self.seen[eng]
        for k, v in ev.items():
            if seen.get(k, 0) < v:
                seen[k] = v
                waits.append((k, v))
        return waits

    def _commit(self, event, reads, writes):
        for r in reads:
            r.r.append(event)
            if len(r.r) > 32:
                d = {}
                for k, v in r.r:
                    if d.get(k, 0) < v:
                        d[k] = v
                r.r = list(d.items())
        for w in writes:
            w.w = event
            w.r = []

    def op(self, eng, fn, reads=(), writes=(), noself=None):
        if noself is None:
            noself = (eng == "pe")
        waits = self._deps(eng, reads, writes, noself)
        self.cnt[eng] += 1
        event = (eng, self.cnt[eng])
        self.ops[eng].append((fn, waits, (eng, 1)))
        self._commit(event, reads, writes)
        self.n_instr += 1
        return event

    def dma(self, eng, fn, dst, reads=(), also=()):
        if dst.dsem is None:
            dst.dsem = "d_" + dst.name
            self.semkeys.append(dst.dsem)
            self.dma_res.append(dst)
        writes = (dst,) + tuple(also)
        waits = self._deps(eng, reads, writes, False)
        dst.dcnt += 16
        event = (dst.dsem, dst.dcnt)
        self.ops[eng].append((fn, waits, (dst.dsem, 16)))
        self._commit(event, reads, writes)
        self.n_instr += 1
        return event

    def barrier(self):
        ev = [(e, self.cnt[e]) for e in self.ENG if self.cnt[e] > 0]
        ev += [(r.dsem, r.dcnt) for r in self.dma_res]
        for eng in self.ENG:
            waits = []
            seen = self.seen[eng]
            for k, v in ev:
                if seen.get(k, 0) < v:
                    seen[k] = v
                    waits.append((k, v))
            if waits:
                self.ops[eng].append((None, waits, None))

    def wait_all(self, eng, resources):
        waits = self._deps(eng, resources, (), False)
        self.ops[eng].append((None, waits, None))

    def emit(self):
        nc = self.nc
        with contextlib.ExitStack() as st:
            sems = {}
            for k in self.semkeys:
                sems[k] = st.enter_context(nc.semaphore("s_" + k))
            block = st.enter_context(nc.Block())

            def run(engname):
                def body(e):
                    for fn, waits, inc in self.ops[engname]:
                        for k, v in waits:
                            e.wait_ge(sems[k], v)
                        if fn is not None:
                            fn(e).then_inc(sems[inc[0]], inc[1])
                return body

            block.tensor(run("pe"))
            block.vector(run("dve"))
            block.scalar(run("act"))
            block.gpsimd(run("pool"))
            block.sync(run("sp"))


class Arena:
    def __init__(self, ap, words):
        self.ap = ap
        self.words = words
        self.off = 0
        self.peak = 0

    def mark(self):
        return self.off

    def release(self, m):
        self.off = m

    def alloc(self, shape, dtype=F32, parts=128):
        free = 1
        for s in shape:
            free *= s
        words = free if dtype == F32 else (free + 1) // 2
        assert self.off + words <= self.words, ("SBUF arena overflow", self.off, words, self.words)
        a = self.ap[0:parts, self.off:self.off + words]
        self.off += words
        self.peak = max(self.peak, self.off)
        if dtype != F32:
            a = a.bitcast(dtype)
        if len(shape) == 2:
            return a.rearrange("p (a b) -> p a b", a=shape[0])
        if len(shape) == 3:
            return a.rearrange("p (a b c) -> p a b c", a=shape[0], b=shape[1])
        if len(shape) == 4:
            return a.rearrange("p (a b c d) -> p a b c d", a=shape[0], b=shape[1], c=shape[2])
        return a


def build_program(TP=TPAD, TILE=512, stages=("ffn1", "mix", "ffn2"), debug=False):
    nc = bass.Bass("TRN2", target_bir_lowering=False)
    dt = nc.dram_tensor

    def din(name, shape):
        return dt(name, shape, F32, kind="ExternalInput").ap()

    xT = din("xT", [NDC, 128, TP])
    vfT = din("vfT", [8, 128, TP])
    pp_d = din("pp", [128, NPP])
    cst_d = din("cst", [128, NCST])
    wg1, wu1, wd1 = din("wg1", [D, DFF]), din("wu1", [D, DFF]), din("wd1", [DFF, D])
    wg2, wu2, wd2 = din("wg2", [D, DFF]), din("wu2", [D, DFF]), din("wd2", [DFF, D])
    win = din("win", [D, PIN])
    wglu = din("wglu", [1024, 1024])
    wups = din("wups", [1024, D])
    wupr = din("wupr", [1024, D])
    wout = din("wout", [D, D])
    wa2 = din("wa2", [128, 1024])
    g2a = din("g2a", [128, 1024])
    g2b = din("g2b", [32, 1024])
    v2d = din("v2", [32, 1024])
    bre_d = din("bre", [128, 8 * 128])
    bim_d = din("bim", [128, 8 * 128])
    bre3_d = din("bre3", [128, 8 * 128])
    bim3_d = din("bim3", [128, 8 * 128])
    cre_d = din("cre", [128, 32 * 64])
    cim_d = din("cim", [128, 32 * 64])
    yT = dt("yT", [NDC, 128, TP], F32, kind="ExternalOutput").ap()
    zT = dt("zT", [NDC, 128, TP], F32, kind="ExternalOutput").ap()
    vfo = dt("vfo", [8, 128, TP], F32, kind="ExternalOutput").ap()
    dbg = dt("dbg", [16, 128, TP], F32, kind="ExternalOutput").ap() if debug else None

    tiles = []
    t0 = 0
    while t0 < TP:
        n = min(TILE, TP - t0)
        tiles.append((t0, n))
        t0 += n

    st = contextlib.ExitStack()
    AW = 53100
    arena_t = st.enter_context(nc.sbuf_tensor("arena", [128, AW], F32))
    ps_t = st.enter_context(nc.psum_tensor("ps", [128, 8, 512], F32))
    A = Arena(arena_t, AW)
    P = Prog(nc)

    Rb = [P.res("bank") for _ in range(8)]
    bank_i = [0]

    def bank():
        i = bank_i[0] % 8
        bank_i[0] += 1
        return ps_t[:, i, :], Rb[i]

    pp = A.alloc([NPP])
    cst = A.alloc([NCST])
    Rpp, Rcst = P.res("pp"), P.res("cst")
    P.dma("sp", lambda e: e.dma_start(out=pp, in_=pp_d[:, :]), Rpp)
    P.dma("sp", lambda e: e.dma_start(out=cst, in_=cst_d[:, :]), Rcst)

    def ppc(name, j=0, parts=128, p0=0):
        o, w = PP[name]
        return pp[p0:p0 + parts, o + j:o + j + 1]

    def ppw(name):
        o, w = PP[name]
        return pp[:, o:o + w]

    def csl(name, parts=128):
        o, w = CS[name]
        return cst[0:parts, o:o + w]

    ones = csl("ones")
    bones = csl("bones")
    identb = A.alloc([128], BF16)
    Rid = P.res("ident")
    P.op("dve", lambda e: e.tensor_copy(out=identb, in_=csl("ident")), [Rcst], [Rid])
    epsT = A.alloc([4])
    Reps = P.res("eps")
    for k_, v_ in enumerate((NORM_EPS, 1e-12, LNX_EPS, 0.0)):
        P.op("dve", lambda e, k_=k_, v_=v_: e.memset(epsT[:, k_:k_ + 1], v_), [], [Reps])
    omka = A.alloc([8])
    Romka = P.res("omka")
    P.op("dve", lambda e: e.tensor_scalar(out=omka, in0=ppw("ka"), scalar1=-1.0, scalar2=1.0, op0=ALU.mult,
                                          op1=ALU.add), [Rpp], [Romka])

    X = A.alloc([NDC, TILE])
    RX = [P.res("X") for _ in range(NDC)]
    hT = A.alloc([NDC, TILE], BF16)
    RhT = P.res("hT")
    rstd = A.alloc([TILE])
    Rrstd = P.res("rstd")
    sqt = [A.alloc([TILE]) for _ in range(2)]
    Rsq = [P.res("sq") for _ in range(2)]

    def dbg_out(idx, ap, res, t0, n, parts=128):
        if dbg is not None:
            P.dma("sp", lambda e: e.dma_start(out=dbg[idx, 0:parts, t0:t0 + n], in_=ap), Rdbg, reads=[res])

    Rdbg = P.res("dbg")

    def rmsnorm_stats(n):
        bk, rb = bank()
        for j in range(NDC):
            s = sqt[j % 2]
            P.op("act", lambda e, j=j, s=s: e.activation(out=s[:, 0:n], in_=X[:, j, 0:n], func=AF.Square),
                 [RX[j]], [Rsq[j % 2]])
            P.op("pe", lambda e, j=j, s=s: e.matmul(bk[:, 0:n], lhsT=ones, rhs=s[:, 0:n], start=(j == 0),
                                                     stop=(j == NDC - 1)), [Rsq[j % 2], Rcst], [rb])
        P.op("act", lambda e: e.activation(out=rstd[:, 0:n], in_=bk[:, 0:n], func=AF.Sqrt, bias=epsT[:, 0:1],
                                           scale=1.0 / D), [rb, Reps], [Rrstd])
        P.op("dve", lambda e: e.reciprocal(out=rstd[:, 0:n], in_=rstd[:, 0:n]), [Rrstd], [Rrstd])

    def norm_apply(gname, n, out_fn, out_res_fn):
        for j in range(NDC):
            P.op("dve", lambda e, j=j: e.scalar_tensor_tensor(out=out_fn(j), in0=X[:, j, 0:n], scalar=ppc(gname, j),
                                                              in1=rstd[:, 0:n], op0=ALU.mult, op1=ALU.mult),
                 [RX[j], Rrstd, Rpp], [out_res_fn(j)])

    def ffn(wg, wu, wd, n):
        m = A.mark()
        actT = A.alloc([NFF, TILE], BF16)
        Ract = [P.res("act") for _ in range(NFF)]
        FG = 1
        wgb = [A.alloc([NDC, FG * 128], BF16) for _ in range(2)]
        wub = [A.alloc([NDC, FG * 128], BF16) for _ in range(2)]
        Rwg = [P.pres("wg%d" % i_) for i_ in range(2)]
        Rwu = [P.pres("wu%d" % i_) for i_ in range(2)]
        wdb = [A.alloc([NFF, 128], BF16) for _ in range(2)]
        Rwd = [P.pres("wd%d" % i_) for i_ in range(2)]
        stmp = [A.alloc([TILE]) for _ in range(2)]
        Rst = [P.res("stmp") for _ in range(2)]
        wgv = wg.rearrange("(j p) c -> p j c", p=128)
        wuv = wu.rearrange("(j p) c -> p j c", p=128)
        wdv = wd.rearrange("(f p) c -> p f c", p=128)
        for g in range(NFF // FG):
            b = g % 2
            c0 = g * FG * 128
            P.dma("pool", lambda e, b=b, c0=c0: e.dma_start(out=wgb[b], in_=wgv[:, :, c0:c0 + FG * 128]), Rwg[b])
            P.dma("pool", lambda e, b=b, c0=c0: e.dma_start(out=wub[b], in_=wuv[:, :, c0:c0 + FG * 128]), Rwu[b])
            for fl in range(FG):
                f = g * FG + fl
                bg, rg = bank()
                bu, ru = bank()
                for j in range(NDC):
                    P.op("pe", lambda e, j=j, b=b, fl=fl, bg=bg: e.matmul(
                        bg[:, 0:n], lhsT=wgb[b][:, j, fl * 128:(fl + 1) * 128], rhs=hT[:, j, 0:n],
                        start=(j == 0), stop=(j == NDC - 1)), [Rwg[b], RhT], [rg])
                for j in range(NDC):
                    P.op("pe", lambda e, j=j, b=b, fl=fl, bu=bu: e.matmul(
                        bu[:, 0:n], lhsT=wub[b][:, j, fl * 128:(fl + 1) * 128], rhs=hT[:, j, 0:n],
                        start=(j == 0), stop=(j == NDC - 1)), [Rwu[b], RhT], [ru])
                s = stmp[f % 2]
                P.op("act", lambda e, s=s, bg=bg: e.activation(out=s[:, 0:n], in_=bg[:, 0:n], func=AF.Silu),
                     [rg], [Rst[f % 2]])
                P.op("dve", lambda e, s=s, bu=bu, f=f: e.tensor_tensor(out=actT[:, f, 0:n], in0=bu[:, 0:n],
                                                                       in1=s[:, 0:n], op=ALU.mult),
                     [ru, Rst[f % 2]], [Ract[f]])
        for d in range(NDC):
            b = d % 2
            P.dma("pool", lambda e, b=b, d=d: e.dma_start(out=wdb[b], in_=wdv[:, :, d * 128:(d + 1) * 128]), Rwd[b])
            bk, rb = bank()
            for f in range(NFF):
                P.op("pe", lambda e, f=f, b=b, bk=bk: e.matmul(bk[:, 0:n], lhsT=wdb[b][:, f, :], rhs=actT[:, f, 0:n],
                                                               start=(f == 0), stop=(f == NFF - 1)),
                     [Rwd[b], Ract[f]], [rb])
            P.op("dve", lambda e, d=d, bk=bk: e.scalar_tensor_tensor(out=X[:, d, 0:n], in0=bk[:, 0:n], scalar=0.5,
                                                                     in1=X[:, d, 0:n], op0=ALU.mult, op1=ALU.add),
                 [rb, RX[d]], [RX[d]])
        P.barrier()
        A.release(m)

    Ry, Rz, Rvfo = P.res("yT"), P.res("zT"), P.res("vfo")
    RXL = P.res("Xload")
    env = dict(locals())
    mixer = make_mixer(env) if "mix" in stages else None
    def do_tile(ti, t0, n):
        src = xT[:, :, t0:t0 + n].rearrange("j p t -> p j t")
        P.dma("sp", lambda e: e.dma_start(out=X[:, :, 0:n], in_=src), RXL, also=RX)
        if "ffn1" in stages:
            rmsnorm_stats(n)
            norm_apply("g1", n, lambda j: hT[:, j, 0:n], lambda j: RhT)
            ffn(wg1, wu1, wd1, n)
        if mixer is not None:
            mixer(ti, t0, n)
        if "ffn2" in stages:
            rmsnorm_stats(n)
            norm_apply("g2", n, lambda j: hT[:, j, 0:n], lambda j: RhT)
            ffn(wg2, wu2, wd2, n)
        dsty = yT[:, :, t0:t0 + n].rearrange("j p t -> p j t")
        P.dma("sp", lambda e: e.dma_start(out=dsty, in_=X[:, :, 0:n]), Ry, reads=RX)
        rmsnorm_stats(n)
        norm_apply("gf", n, lambda j: X[:, j, 0:n], lambda j: RX[j])
        dstz = zT[:, :, t0:t0 + n].rearrange("j p t -> p j t")
        P.dma("sp", lambda e: e.dma_start(out=dstz, in_=X[:, :, 0:n]), Rz, reads=RX)

    for ti, (t0, n) in enumerate(tiles):
        do_tile(ti, t0, n)
    P.wait_all("sp", [Ry, Rz, Rvfo, Rdbg])
    P.emit()
    st.close()
    nc._prog_stats = (P.n_instr, A.peak)
    return nc


def make_mixer(env):
    g = dict(env)
    nc, P, A = g["nc"], g["P"], g["A"]
    X, RX, hT, RhT, rstd, Rrstd = g["X"], g["RX"], g["hT"], g["RhT"], g["rstd"], g["Rrstd"]
    pp, Rpp, cst, Rcst, ppc, ppw, csl = g["pp"], g["Rpp"], g["cst"], g["Rcst"], g["ppc"], g["ppw"], g["csl"]
    ones, bones, identb, Rid, epsT, Reps = g["ones"], g["bones"], g["identb"], g["Rid"], g["epsT"], g["Reps"]
    omka, Romka = g["omka"], g["Romka"]
    ps_t, Rb = g["ps_t"], g["Rb"]
    TILE = g["TILE"]
    win, wglu, wups, wupr, wout = g["win"], g["wglu"], g["wups"], g["wupr"], g["wout"]
    vfT, vfo, Rvfo = g["vfT"], g["vfo"], g["Rvfo"]
    dbg_out = g["dbg_out"]
    rmsnorm_stats, norm_apply = g["rmsnorm_stats"], g["norm_apply"]

    def PB(i):
        return ps_t[:, i, :]

    wa2b = A.alloc([1024], BF16)
    g2ab = A.alloc([1024], BF16)
    g2bb = A.alloc([1024], BF16)
    v2b = A.alloc([1024], BF16)
    Rlw = [P.res("loraw") for _ in range(4)]
    P.dma("pool", lambda e: e.dma_start(out=wa2b, in_=g["wa2"][:, :]), Rlw[0])
    P.dma("pool", lambda e: e.dma_start(out=g2ab, in_=g["g2a"][:, :]), Rlw[1])
    P.dma("pool", lambda e: e.dma_start(out=g2bb[0:32, :], in_=g["g2b"][:, :]), Rlw[2])
    P.dma("pool", lambda e: e.dma_start(out=v2b[0:32, :], in_=g["v2d"][:, :]), Rlw[3])
    bstb = [A.alloc([8, 128], BF16) for _ in range(2)]
    Rbst = [P.res("bst") for _ in range(2)]
    P.dma("pool", lambda e: e.dma_start(out=bstb[0], in_=g["bre_d"].rearrange("p (a b) -> p a b", a=8)), Rbst[0])
    P.dma("pool", lambda e: e.dma_start(out=bstb[1], in_=g["bim_d"].rearrange("p (a b) -> p a b", a=8)), Rbst[1])
    bst3 = [A.alloc([8, 128], BF16) for _ in range(2)]
    P.dma("pool", lambda e: e.dma_start(out=bst3[0], in_=g["bre3_d"].rearrange("p (a b) -> p a b", a=8)), Rbst[0])
    P.dma("pool", lambda e: e.dma_start(out=bst3[1], in_=g["bim3_d"].rearrange("p (a b) -> p a b", a=8)), Rbst[1])
    cqb = [A.alloc([32, 64], BF16) for _ in range(2)]
    Rcq = P.res("cq")

    ST32 = A.alloc([8, 64])
    STb = A.alloc([8, 64], BF16)
    RST = [P.res("ST") for _ in range(8)]
    RSTb = [P.res("STb") for _ in range(8)]
    P.op("dve", lambda e: e.memset(ST32, 0.0), [], RST)
    P.op("dve", lambda e: e.memset(STb, 0.0), [], RSTb)
    carry = A.alloc([32])
    Rcar = [P.res("carry") for _ in range(32)]
    P.op("dve", lambda e: e.memset(carry, 0.0), [], Rcar)
    s5s = [A.alloc([32]) for _ in range(2)]
    Rs5 = P.res("s5state")
    P.op("dve", lambda e: e.memset(s5s[0], 0.0), [], [Rs5])
    P.op("dve", lambda e: e.memset(s5s[1], 0.0), [Rs5], [Rs5])

    rho = A.alloc([32])
    rho0 = A.alloc([32, 64])
    tcs = A.alloc([32, 64])
    tsn = A.alloc([32, 64])
    Rtab = P.res("s5tab")
    m0 = A.mark()
    c32 = [A.alloc([32, 64]) for _ in range(2)]
    Rc32 = [P.res("c32") for _ in range(2)]
    P.dma("sp", lambda e: e.dma_start(out=c32[0], in_=g["cre_d"].rearrange("p (a b) -> p a b", a=32)), Rc32[0])
    P.dma("sp", lambda e: e.dma_start(out=c32[1], in_=g["cim_d"].rearrange("p (a b) -> p a b", a=32)), Rc32[1])
    sc = [A.alloc([32]) for _ in range(12)]
    Rsc = P.res("s5scratch")
    lre, lim, ldt = ppw("lre"), ppw("lim"), ppw("ldt")
    dtv, th, k_, r1, sn1, cs1, nr, den, qre, qim, t1, t2 = sc
    TWO_PI = 2.0 * math.pi

    def dv(fn, rd=(), wr=None):
        P.op("dve", fn, [Rpp, Rsc] + list(rd), [Rsc] if wr is None else wr)

    def ac(fn, rd=(), wr=None):
        P.op("act", fn, [Rpp, Rsc] + list(rd), [Rsc] if wr is None else wr)

    ac(lambda e: e.activation(out=dtv, in_=ldt, func=AF.Exp))
    dv(lambda e: e.tensor_tensor(out=t1, in0=lre, in1=dtv, op=ALU.mult))
    ac(lambda e: e.activation(out=rho, in_=t1, func=AF.Exp), wr=[Rsc, Rtab])
    dv(lambda e: e.tensor_tensor(out=th, in0=lim, in1=dtv, op=ALU.mult))

    def sin_reduced(out, src, shift):
        i32 = t2.bitcast(mybir.dt.int32)
        dv(lambda e: e.tensor_scalar(out=k_, in0=src, scalar1=shift, scalar2=1.0 / TWO_PI, op0=ALU.add, op1=ALU.mult))
        dv(lambda e: e.tensor_copy(out=i32, in_=k_))
        dv(lambda e: e.tensor_copy(out=k_, in_=i32))
        dv(lambda e: e.scalar_tensor_tensor(out=r1, in0=k_, scalar=-TWO_PI, in1=src, op0=ALU.mult, op1=ALU.add))
        dv(lambda e: e.tensor_scalar(out=r1, in0=r1, scalar1=shift, scalar2=None, op0=ALU.add))
        dv(lambda e: e.tensor_scalar(out=k_, in0=r1, scalar1=math.pi, scalar2=-TWO_PI, op0=ALU.is_gt, op1=ALU.mult))
        dv(lambda e: e.tensor_tensor(out=r1, in0=r1, in1=k_, op=ALU.add))
        dv(lambda e: e.tensor_scalar(out=k_, in0=r1, scalar1=-math.pi, scalar2=TWO_PI, op0=ALU.is_lt, op1=ALU.mult))
        dv(lambda e: e.tensor_tensor(out=r1, in0=r1, in1=k_, op=ALU.add))
        dv(lambda e: e.tensor_scalar(out=r1, in0=r1, scalar1=-math.pi, scalar2=math.pi, op0=ALU.max, op1=ALU.min))
        ac(lambda e: e.activation(out=out, in_=r1, func=AF.Sin))

    sin_reduced(sn1, th, 0.0)
    sin_reduced(cs1, th, math.pi / 2)
    dv(lambda e: e.tensor_tensor(out=nr, in0=rho, in1=cs1, op=ALU.mult), rd=[Rtab])
    dv(lambda e: e.tensor_scalar(out=nr, in0=nr, scalar1=-1.0, scalar2=None, op0=ALU.add))
    dv(lambda e: e.tensor_tensor(out=t1, in0=rho, in1=sn1, op=ALU.mult), rd=[Rtab])
    dv(lambda e: e.tensor_tensor(out=den, in0=lre, in1=lre, op=ALU.mult))
    dv(lambda e: e.tensor_tensor(out=t2, in0=lim, in1=lim, op=ALU.mult))
    dv(lambda e: e.tensor_tensor(out=den, in0=den, in1=t2, op=ALU.add))
    dv(lambda e: e.reciprocal(out=den, in_=den))
    dv(lambda e: e.tensor_tensor(out=qre, in0=nr, in1=lre, op=ALU.mult))
    dv(lambda e: e.tensor_tensor(out=t2, in0=t1, in1=lim, op=ALU.mult))
    dv(lambda e: e.tensor_tensor(out=qre, in0=qre, in1=t2, op=ALU.add))
    dv(lambda e: e.tensor_tensor(out=qre, in0=qre, in1=den, op=ALU.mult))
    dv(lambda e: e.tensor_tensor(out=qim, in0=t1, in1=lre, op=ALU.mult))
    dv(lambda e: e.tensor_tensor(out=t2, in0=nr, in1=lim, op=ALU.mult))
    dv(lambda e: e.tensor_tensor(out=qim, in0=qim, in1=t2, op=ALU.subtract))
    dv(lambda e: e.tensor_tensor(out=qim, in0=qim, in1=den, op=ALU.mult))
    ctmp = [A.alloc([32, 64]) for _ in range(2)]
    qreb = qre.unsqueeze(2).to_broadcast([128, 32, 64])
    qimb = qim.unsqueeze(2).to_broadcast([128, 32, 64])
    dv(lambda e: e.tensor_tensor(out=ctmp[0], in0=c32[0], in1=qreb, op=ALU.mult), rd=Rc32)
    dv(lambda e: e.tensor_tensor(out=ctmp[1], in0=c32[1], in1=qimb, op=ALU.mult), rd=Rc32)
    dv(lambda e: e.tensor_tensor(out=cqb[0], in0=ctmp[0], in1=ctmp[1], op=ALU.subtract), wr=[Rsc, Rcq])
    dv(lambda e: e.tensor_tensor(out=ctmp[0], in0=c32[0], in1=qimb, op=ALU.mult), rd=Rc32)
    dv(lambda e: e.tensor_tensor(out=ctmp[1], in0=c32[1], in1=qreb, op=ALU.mult), rd=Rc32)
    dv(lambda e: e.tensor_tensor(out=ctmp[0], in0=ctmp[0], in1=ctmp[1], op=ALU.add))
    dv(lambda e: e.tensor_scalar(out=cqb[1], in0=ctmp[0], scalar1=-1.0, scalar2=None, op0=ALU.mult), wr=[Rsc, Rcq])
    dv(lambda e: e.tensor_copy(out=tcs[:, :, 0:1], in_=cs1.unsqueeze(2)), wr=[Rsc, Rtab])
    dv(lambda e: e.tensor_copy(out=tsn[:, :, 0:1], in_=sn1.unsqueeze(2)), wr=[Rsc, Rtab])
    tt = [A.alloc([32, 32]) for _ in range(2)]
    mlen = 1
    while mlen < 64:
        cm = tcs[:, :, mlen - 1:mlen].to_broadcast([128, 32, mlen])
        sm = tsn[:, :, mlen - 1:mlen].to_broadcast([128, 32, mlen])
        lo_c, lo_s = tcs[:, :, 0:mlen], tsn[:, :, 0:mlen]
        hi_c, hi_s = tcs[:, :, mlen:2 * mlen], tsn[:, :, mlen:2 * mlen]
        a0, a1 = tt[0][:, :, 0:mlen], tt[1][:, :, 0:mlen]
        dv(lambda e, lo_c=lo_c, cm=cm, a0=a0: e.tensor_tensor(out=a0, in0=lo_c, in1=cm, op=ALU.mult), rd=[Rtab])
        dv(lambda e, lo_s=lo_s, sm=sm, a1=a1: e.tensor_tensor(out=a1, in0=lo_s, in1=sm, op=ALU.mult), rd=[Rtab])
        dv(lambda e, hi_c=hi_c, a0=a0, a1=a1: e.tensor_tensor(out=hi_c, in0=a0, in1=a1, op=ALU.subtract), rd=[Rtab], wr=[Rsc, Rtab])
        dv(lambda e, lo_c=lo_c, sm=sm, a0=a0: e.tensor_tensor(out=a0, in0=lo_c, in1=sm, op=ALU.mult), rd=[Rtab])
        dv(lambda e, lo_s=lo_s, cm=cm, a1=a1: e.tensor_tensor(out=a1, in0=lo_s, in1=cm, op=ALU.mult), rd=[Rtab])
        dv(lambda e, hi_s=hi_s, a0=a0, a1=a1: e.tensor_tensor(out=hi_s, in0=a0, in1=a1, op=ALU.add), rd=[Rtab], wr=[Rsc, Rtab])
        mlen *= 2
    for hf in range(4):
        gsl = slice(8 * hf, 8 * hf + 8)
        dv(lambda e, gsl=gsl: e.tensor_copy(out=rho0[:, gsl, :].rearrange("p (q j) t -> p q j t", q=4),
                                            in_=rho[:, gsl].rearrange("p (j q) -> p q j", q=4).unsqueeze(3).to_broadcast([128, 4, 2, 64])),
           rd=[Rtab], wr=[Rsc, Rtab])
    dv(lambda e: e.memset(rho0[:, :, 0:1], 0.0), rd=[Rtab], wr=[Rsc, Rtab])
    P.barrier()
    A.release(m0)

    wbuf = [None] * 3
    Rwb = [None] * 3
    wbi = [0]
    winv = win.rearrange("(j p) c -> p j c", p=128)

    def proj(col0, ncols, n, bk, rb):
        b = wbi[0] % 3
        wbi[0] += 1
        wb_ = wbuf[b]
        P.dma("pool", lambda e: e.dma_start(out=wb_[:, :, 0:ncols], in_=winv[:, :, col0:col0 + ncols]), Rwb[b])
        for j in range(NDC):
            P.op("pe", lambda e, j=j: e.matmul(bk[0:ncols, 0:n], lhsT=wb_[:, j, 0:ncols], rhs=hT[:, j, 0:n],
                                               start=(j == 0), stop=(j == NDC - 1)), [Rwb[b], RhT], [rb])

    stage = [None] * 2
    Rstg = [None] * 2
    dtmp = [None] * 2
    Rdt = [None] * 2
    sti = [0]

    def shifted(col0, ncols, n, mu_ap, ci, out_ap, out_res):
        s = sti[0] % 2
        sti[0] += 1
        bi = (wbi[0]) % 2
        bk, rb = PB(bi), Rb[bi]
        proj(col0, ncols, n, bk, rb)
        sg, rs = stage[s], Rstg[s]
        dt_, rdt_ = dtmp[s], Rdt[s]
        P.op("act", lambda e: e.activation(out=sg[0:ncols, 1:n + 1], in_=bk[0:ncols, 0:n], func=AF.Copy), [rb], [rs])
        P.op("dve", lambda e: e.tensor_copy(out=sg[0:ncols, 0:1], in_=carry[0:ncols, ci:ci + 1]), [Rcar[ci], rs], [rs])
        P.op("dve", lambda e: e.tensor_tensor(out=dt_[0:ncols, 0:n], in0=sg[0:ncols, 0:n], in1=sg[0:ncols, 1:n + 1],
                                              op=ALU.subtract), [rs], [rdt_])
        P.op("dve", lambda e: e.scalar_tensor_tensor(out=out_ap, in0=dt_[0:ncols, 0:n], scalar=mu_ap,
                                                     in1=sg[0:ncols, 1:n + 1], op0=ALU.mult, op1=ALU.add),
             [rdt_, rs, Rpp], [out_res])
        P.op("dve", lambda e: e.tensor_copy(out=carry[0:ncols, ci:ci + 1], in_=sg[0:ncols, n:n + 1]), [rs], [Rcar[ci]])

    def mixer(ti, t0, n):
        nch = n // 64
        mk = A.mark()
        for b_ in range(3):
            wbuf[b_] = A.alloc([NDC, 128], BF16)
            Rwb[b_] = P.pres("wbuf%d" % b_)
        for b_ in range(2):
            stage[b_] = A.alloc([TILE + 1])
            Rstg[b_] = P.res("stage")
            dtmp[b_] = A.alloc([TILE])
            Rdt[b_] = P.res("dtmp")
        rmsnorm_stats(n)
        norm_apply("gm", n, lambda j: hT[:, j, 0:n], lambda j: RhT)
        ybT = A.alloc([8, TILE], BF16)
        yaT = A.alloc([8, TILE], BF16)
        Ryb = [P.res("yb") for _ in range(8)]
        Rya = [P.res("ya") for _ in range(8)]

        mr = A.mark()
        xwa = A.alloc([TILE])
        wab = A.alloc([TILE], BF16)
        sg1b = A.alloc([TILE], BF16)
        sg2b = A.alloc([TILE], BF16)
        xvrb = A.alloc([TILE], BF16)
        tmpg = A.alloc([TILE])
        Rl = [P.res("lora") for _ in range(6)]
        shifted(COL_WA, 128, n, ppc("muwa"), 24, xwa[:, 0:n], Rl[0])
        P.op("act", lambda e: e.activation(out=wab[0:64, 0:n], in_=xwa[0:64, 0:n], func=AF.Tanh), [Rl[0]], [Rl[1]])
        P.op("act", lambda e: e.activation(out=wab[64:128, 0:n], in_=xwa[64:128, 0:n], func=AF.Copy), [Rl[0]], [Rl[1]])
        shifted(COL_G, 128, n, ppc("mug1"), 25, tmpg[:, 0:n], Rl[5])
        P.op("act", lambda e: e.activation(out=sg1b[:, 0:n], in_=tmpg[:, 0:n], func=AF.Sigmoid), [Rl[5]], [Rl[2]])
        shifted(COL_G + 128, 32, n, ppc("mug2", parts=32), 26, tmpg[0:32, 0:n], Rl[5])
        P.op("act", lambda e: e.activation(out=sg2b[0:32, 0:n], in_=tmpg[0:32, 0:n], func=AF.Sigmoid), [Rl[5]], [Rl[3]])
        shifted(COL_VR, 32, n, ppc("muvr", parts=32), 27, tmpg[0:32, 0:n], Rl[5])
        P.op("act", lambda e: e.activation(out=xvrb[0:32, 0:n], in_=tmpg[0:32, 0:n], func=AF.Copy), [Rl[5]], [Rl[4]])

        if MIX_STOP == 1:
            P.barrier(); A.release(mk); return
        f32n = ["xr", "xk", "xv", "sg", "iclr", "gate", "sv", "vf", "v32", "kkn", "tt", "kmod", "bonus", "cl", "pm",
                "pinv", "pprev", "t1", "t2", "OT"]
        alias = {"OT": "xv", "cl": "xv", "bonus": "xk", "tt": "pprev", "sv": "pinv", "vf": "pm"}
        W = {nm: A.alloc([TILE]) for nm in f32n if nm not in alias and nm not in ("t1", "t2")}
        R = {nm: (P.pres("pm_vf") if nm == "pm" else P.res(nm)) for nm in f32n if nm not in alias and nm not in ("t1", "t2")}
        W["t1"], R["t1"] = xwa, Rl[0]
        W["t2"], R["t2"] = tmpg, Rl[5]
        for k_a, v_a in alias.items():
            W[k_a] = W[v_a]
            R[k_a] = R[v_a]
        arb = A.alloc([8, 2, 64], BF16)
        btb = A.alloc([TILE], BF16)
        ktb = A.alloc([TILE], BF16)
        bhb = A.alloc([TILE], BF16)
        khb = A.alloc([TILE], BF16)
        vbb = A.alloc([TILE], BF16)
        Rar, Rbt, Rkt, Rbh, Rkh, Rvb = [P.res(x) for x in ("ar", "bt", "kt", "bh", "kh", "vb")]
        tok = A.alloc([4, 3, 128], BF16, parts=64)
        A1 = A.alloc([2, 4, 256], BF16, parts=64)
        Qb = [A.alloc([2, 4, 64], BF16, parts=64) for _ in range(2)]
        QTb = [A.alloc([2, 4, 64], BF16, parts=64) for _ in range(2)]
        Tm32 = A.alloc([2, 4, 64], parts=64)
        Tmb = A.alloc([2, 4, 64], BF16, parts=64)
        Xb = A.alloc([128], BF16, parts=64)
        Ub = A.alloc([128], BF16, parts=64)
        Xs = A.alloc([128], parts=64)
        RXs = P.res("Xs")
        Rtok, RA1, RTm32, RTmb, RXb, RUb = [P.res(x) for x in ("tok", "A1", "Tm32", "Tmb", "Xb", "Ub")]
        RQ = [P.res("Q") for _ in range(2)]
        RQT = [P.res("QT") for _ in range(2)]
        m1c = csl("m1", 64)
        mslc = csl("msl", 64)
        ifc = csl("ifree", 64)
        rmask = csl("rmask")

        def dvo(fn, rd, wr):
            P.op("dve", fn, rd, wr)

        def aco(fn, rd, wr):
            P.op("act", fn, rd, wr)

        for c in range(8):
            cols = slice(128 * c, 128 * c + 128)
            shifted(COL_R + 128 * c, 128, n, ppc("mur", c), c, W["xr"][:, 0:n], R["xr"])
            shifted(COL_K + 128 * c, 128, n, ppc("muk", c), 8 + c, W["xk"][:, 0:n], R["xk"])
            shifted(COL_V + 128 * c, 128, n, ppc("muv", c), 16 + c, W["xv"][:, 0:n], R["xv"])
            P.dma("sp", lambda e, c=c: e.dma_start(out=vfo[c, :, t0:t0 + n], in_=W["xv"][:, 0:n]), Rvfo, reads=[R["xv"]])
            P.dma("sp", lambda e, c=c: e.dma_start(out=W["vf"][:, 0:n], in_=vfT[c, :, t0:t0 + n]), R["vf"])
            P.op("pe", lambda e, cols=cols: e.matmul(PB(2)[:, 0:n], lhsT=wa2b[0:64, cols], rhs=wab[0:64, 0:n], start=True,
                                                     stop=True), [Rlw[0], Rl[1]], [Rb[2]])
            aco(lambda e, c=c: e.activation(out=W["sg"][:, 0:n], in_=PB(2)[:, 0:n], func=AF.Sigmoid, bias=ppc("w0", c)),
                [Rb[2], Rpp], [R["sg"]])
            P.op("pe", lambda e, cols=cols: e.matmul(PB(3)[:, 0:n], lhsT=wa2b[64:128, cols], rhs=wab[64:128, 0:n],
                                                     start=True, stop=True), [Rlw[0], Rl[1]], [Rb[3]])
            aco(lambda e, c=c: e.activation(out=W["iclr"][:, 0:n], in_=PB(3)[:, 0:n], func=AF.Sigmoid, bias=ppc("a0", c)),
                [Rb[3], Rpp], [R["iclr"]])
            P.op("pe", lambda e, cols=cols: e.matmul(PB(2)[:, 0:n], lhsT=g2ab[:, cols], rhs=sg1b[:, 0:n], start=True,
                                                     stop=False), [Rlw[1], Rl[2]], [Rb[2]])
            P.op("pe", lambda e, cols=cols: e.matmul(PB(2)[:, 0:n], lhsT=g2bb[0:32, cols], rhs=sg2b[0:32, 0:n], start=False,
                                                     stop=True), [Rlw[2], Rl[3]], [Rb[2]])
            aco(lambda e: e.activation(out=W["gate"][:, 0:n], in_=PB(2)[:, 0:n], func=AF.Copy), [Rb[2]], [R["gate"]])
            P.op("pe", lambda e, cols=cols: e.matmul(PB(3)[:, 0:n], lhsT=v2b[0:32, cols], rhs=xvrb[0:32, 0:n], start=True,
                                                     stop=True), [Rlw[3], Rl[4]], [Rb[3]])
            aco(lambda e, c=c: e.activation(out=W["sv"][:, 0:n], in_=PB(3)[:, 0:n], func=AF.Sigmoid, bias=ppc("v0", c)),
                [Rb[3], Rpp], [R["sv"]])
            dvo(lambda e: e.tensor_tensor(out=W["t1"][:, 0:n], in0=W["vf"][:, 0:n], in1=W["xv"][:, 0:n], op=ALU.subtract),
                [R["vf"], R["xv"]], [R["t1"]])
            dvo(lambda e: e.tensor_tensor(out=W["t1"][:, 0:n], in0=W["t1"][:, 0:n], in1=W["sv"][:, 0:n], op=ALU.mult),
                [R["t1"], R["sv"]], [R["t1"]])
            dvo(lambda e: e.scalar_tensor_tensor(out=W["v32"][:, 0:n], in0=W["t1"][:, 0:n], scalar=ppc("flag"),
                                                 in1=W["xv"][:, 0:n], op0=ALU.mult, op1=ALU.add),
                [R["t1"], R["xv"], Rpp], [R["v32"]])
            aco(lambda e: e.activation(out=vbb[:, 0:n], in_=W["v32"][:, 0:n], func=AF.Copy), [R["v32"]], [Rvb])
            dvo(lambda e, c=c: e.tensor_scalar(out=W["t1"][:, 0:n], in0=W["xk"][:, 0:n], scalar1=ppc("kk", c), scalar2=None,
                                               op0=ALU.mult), [R["xk"], Rpp, R["t1"]], [R["t1"]])
            aco(lambda e: e.activation(out=W["t2"][:, 0:n], in_=W["t1"][:, 0:n], func=AF.Square), [R["t1"]], [R["t2"]])
            P.op("pe", lambda e: e.matmul(PB(2)[:, 0:n], lhsT=bones, rhs=W["t2"][:, 0:n], start=True, stop=True),
                 [Rcst, R["t2"]], [Rb[2]])
            aco(lambda e: e.activation(out=W["t2"][:, 0:n], in_=PB(2)[:, 0:n], func=AF.Sqrt, bias=epsT[:, 1:2]),
                [Rb[2], Reps], [R["t2"]])
            dvo(lambda e: e.reciprocal(out=W["t2"][:, 0:n], in_=W["t2"][:, 0:n]), [R["t2"]], [R["t2"]])
            dvo(lambda e: e.tensor_tensor(out=W["kkn"][:, 0:n], in0=W["t1"][:, 0:n], in1=W["t2"][:, 0:n], op=ALU.mult),
                [R["t1"], R["t2"]], [R["kkn"]])
            dvo(lambda e, c=c: e.tensor_scalar(out=W["tt"][:, 0:n], in0=W["iclr"][:, 0:n], scalar1=ppc("ka", c),
                                               scalar2=omka[:, c:c + 1], op0=ALU.mult, op1=ALU.add),
                [R["iclr"], Rpp, Romka], [R["tt"]])
            dvo(lambda e: e.tensor_tensor(out=W["kmod"][:, 0:n], in0=W["xk"][:, 0:n], in1=W["tt"][:, 0:n], op=ALU.mult),
                [R["xk"], R["tt"]], [R["kmod"]])
            dvo(lambda e, c=c: e.scalar_tensor_tensor(out=W["t1"][:, 0:n], in0=W["xr"][:, 0:n], scalar=ppc("rk", c),
                                                      in1=W["kmod"][:, 0:n], op0=ALU.mult, op1=ALU.mult),
                [R["xr"], R["kmod"], Rpp, R["t1"]], [R["t1"]])
            P.op("pe", lambda e: e.matmul(PB(3)[:, 0:n], lhsT=bones, rhs=W["t1"][:, 0:n], start=True, stop=True),
                 [Rcst, R["t1"]], [Rb[3]])
            dvo(lambda e: e.tensor_tensor(out=W["bonus"][:, 0:n], in0=PB(3)[:, 0:n], in1=W["v32"][:, 0:n], op=ALU.mult),
                [Rb[3], R["v32"]], [R["bonus"]])
            dvo(lambda e: e.tensor_tensor_scan(out=W["cl"][:, 0:n], data0=rmask[:, 0:n], data1=W["sg"][:, 0:n], initial=0.0,
                                               op0=ALU.mult, op1=ALU.add), [Rcst, R["sg"]], [R["cl"]])
            aco(lambda e: e.activation(out=W["pm"][:, 0:n], in_=W["cl"][:, 0:n], func=AF.Exp, scale=-C0), [R["cl"]], [R["pm"]])
            aco(lambda e: e.activation(out=W["pinv"][:, 0:n], in_=W["cl"][:, 0:n], func=AF.Exp, scale=C0), [R["cl"]], [R["pinv"]])
            dvo(lambda e: e.tensor_tensor(out=W["t2"][:, 0:n], in0=W["cl"][:, 0:n], in1=W["sg"][:, 0:n], op=ALU.subtract),
                [R["cl"], R["sg"], R["t2"]], [R["t2"]])
            aco(lambda e: e.activation(out=W["pprev"][:, 0:n], in_=W["t2"][:, 0:n], func=AF.Exp, scale=-C0),
                [R["t2"]], [R["pprev"]])
            v3 = lambda ap: ap[:, 0:n].rearrange("p (c l) -> p c l", l=64)
            dvo(lambda e: e.scalar_tensor_tensor(out=arb[:, 0:nch, 0, :], in0=v3(W["kkn"]), scalar=-1.0, in1=v3(W["pprev"]),
                                                 op0=ALU.mult, op1=ALU.mult), [R["kkn"], R["pprev"]], [Rar])
            dvo(lambda e: e.tensor_tensor(out=arb[:, 0:nch, 1, :], in0=v3(W["xr"]), in1=v3(W["pm"]), op=ALU.mult),
                [R["xr"], R["pm"], Rar], [Rar])
            dvo(lambda e: e.tensor_tensor(out=W["t1"][:, 0:n], in0=W["kkn"][:, 0:n], in1=W["iclr"][:, 0:n], op=ALU.mult),
                [R["kkn"], R["iclr"], R["t1"]], [R["t1"]])
            dvo(lambda e: e.tensor_tensor(out=btb[:, 0:n], in0=W["t1"][:, 0:n], in1=W["pinv"][:, 0:n], op=ALU.mult),
                [R["t1"], R["pinv"]], [Rbt])
            dvo(lambda e: e.tensor_tensor(out=ktb[:, 0:n], in0=W["kmod"][:, 0:n], in1=W["pinv"][:, 0:n], op=ALU.mult),
                [R["kmod"], R["pinv"]], [Rkt])
            plb = v3(W["pm"])[:, :, 63:64].to_broadcast([128, nch, 64])
            dvo(lambda e: e.tensor_tensor(out=v3(bhb), in0=v3(btb), in1=plb, op=ALU.mult), [Rbt, R["pm"]], [Rbh])
            dvo(lambda e: e.tensor_tensor(out=v3(khb), in0=v3(ktb), in1=plb, op=ALU.mult), [Rkt, R["pm"]], [Rkh])

            if MIX_STOP == 2:
                continue
            for q0 in range(0, nch, 4):
                nq = min(4, nch - q0)
                for i in range(nq):
                    cc = slice((q0 + i) * 64, (q0 + i) * 64 + 64)
                    pbf = PB(6 + i // 2).bitcast(BF16)[0:64, 0:768].rearrange("p (a b c) -> p a b c", a=2, b=3)
                    for k3, (src, rs) in enumerate(((bhb, Rbh), (khb, Rkh), (vbb, Rvb))):
                        P.op("pe", lambda e, src=src, cc=cc, pbf=pbf, i=i, k3=k3: e.transpose(
                            pbf[:, i % 2, k3, :], src[:, cc], identb), [rs, Rid], [Rb[6 + i // 2]])
                for hb in range((nq + 1) // 2):
                    nn = min(2, nq - 2 * hb)
                    pbf = PB(6 + hb).bitcast(BF16)[0:64, 0:768].rearrange("p (a b c) -> p a b c", a=2, b=3)
                    aco(lambda e, hb=hb, nn=nn, pbf=pbf: e.activation(out=tok[:, 2 * hb:2 * hb + nn], in_=pbf[:, 0:nn],
                                                                       func=AF.Copy), [Rb[6 + hb]], [Rtok])
                for i in range(nq):
                    ci = q0 + i
                    cc = slice(ci * 64, ci * 64 + 64)
                    for h2 in range(2):
                        hs = slice(64 * h2, 64 * h2 + 64)
                        bka = 2 * h2 + i // 2
                        pa = PB(bka)[0:64, (i % 2) * 256:(i % 2) * 256 + 256].rearrange("p (x y) -> p x y", x=2)
                        pn = PB(4 + h2)[0:64, i * 64:i * 64 + 64]
                        rhs_ar = arb[hs, ci, :, :].rearrange("p a b -> p (a b)")
                        P.op("pe", lambda e, hs=hs, cc=cc, pa=pa, rhs_ar=rhs_ar: e.matmul(
                            pa[:, 0, :], lhsT=btb[hs, cc], rhs=rhs_ar, start=True, stop=True), [Rbt, Rar], [Rb[bka]])
                        P.op("pe", lambda e, hs=hs, cc=cc, pa=pa, rhs_ar=rhs_ar: e.matmul(
                            pa[:, 1, :], lhsT=ktb[hs, cc], rhs=rhs_ar, start=True, stop=True), [Rkt, Rar], [Rb[bka]])
                        P.op("pe", lambda e, hs=hs, cc=cc, pn=pn, ci=ci: e.matmul(
                            pn, lhsT=arb[hs, ci, 0, :], rhs=btb[hs, cc], start=True, stop=True), [Rbt, Rar], [Rb[4 + h2]])
                for h2 in range(2):
                    dvo(lambda e, h2=h2, nq=nq: e.tensor_tensor(
                        out=A1[:, h2, 0:nq, :],
                        in0=ps_t[0:64, 2 * h2:2 * h2 + 2, :].rearrange("p b (a x) -> p (b a) x", a=2)[:, 0:nq, :],
                        in1=m1c[:, 0:256].unsqueeze(1).to_broadcast([64, nq, 256]), op=ALU.mult),
                        [Rb[2 * h2], Rb[2 * h2 + 1], Rcst, RA1], [RA1])
                dvo(lambda e, nq=nq: e.tensor_tensor(
                    out=Qb[0][:, :, 0:nq, :], in0=ps_t[0:64, 4:6, 0:256].rearrange("p h (i t) -> p h i t", i=4)[:, :, 0:nq, :],
                    in1=mslc[:, 0:64].unsqueeze(1).unsqueeze(1).to_broadcast([64, 2, nq, 64]), op=ALU.mult),
                    [Rb[4], Rb[5], Rcst], [RQ[0]])
                A1v = A1.rearrange("p h i (x y t) -> p h i x y t", x=2, y=2)
                ntv = A1v[:, :, :, 0, 0, :]
                arbT = A1v[:, :, :, 0, 1, :]
                aktT = A1v[:, :, :, 1, 0, :]
                arkT = A1v[:, :, :, 1, 1, :]
                ifb = ifc[:, 0:64].unsqueeze(1).unsqueeze(1).to_broadcast([64, 2, nq, 64])
                sq_ = lambda ap, nq=nq: ap[:, :, 0:nq, :]
                dvo(lambda e, nq=nq: e.tensor_copy(out=sq_(QTb[0]), in_=ntv[:, :, 0:nq, :]), [RA1], [RQT[0]])
                dvo(lambda e, ifb=ifb: e.tensor_tensor(out=sq_(Tm32), in0=sq_(QTb[0]), in1=ifb, op=ALU.add), [RQT[0], Rcst], [RTm32])
                aco(lambda e: e.activation(out=sq_(Tmb), in_=sq_(Tm32), func=AF.Copy), [RTm32], [RTmb])
                cur = 0
                bv = lambda k: PB(k)[0:64, :].rearrange("p (h i t) -> p h i t", h=2, i=4)
                for lvl in range(5):
                    nxt = 1 - cur
                    last = (lvl == 4)
                    pq, pqt, pt = bv(0), bv(1), bv(2)
                    Qc, QTc, Qn, QTn = Qb[cur], QTb[cur], Qb[nxt], QTb[nxt]
                    for i in range(nq):
                        for h2 in range(2):
                            P.op("pe", lambda e, i=i, h2=h2, pq=pq, Qc=Qc, QTc=QTc: e.matmul(
                                pq[:, h2, i, :], lhsT=QTc[:, h2, i, :], rhs=Qc[:, h2, i, :], start=True, stop=True),
                                [RQ[cur], RQT[cur]], [Rb[0]])
                    if not last:
                        for i in range(nq):
                            for h2 in range(2):
                                P.op("pe", lambda e, i=i, h2=h2, pqt=pqt, Qc=Qc, QTc=QTc: e.matmul(
                                    pqt[:, h2, i, :], lhsT=Qc[:, h2, i, :], rhs=QTc[:, h2, i, :], start=True, stop=True),
                                    [RQ[cur], RQT[cur]], [Rb[1]])
                    aco(lambda e, Qn=Qn, pq=pq: e.activation(out=sq_(Qn), in_=sq_(pq), func=AF.Copy), [Rb[0]], [RQ[nxt]])
                    if not last:
                        dvo(lambda e, QTn=QTn, pqt=pqt: e.tensor_copy(out=sq_(QTn), in_=sq_(pqt)), [Rb[1]], [RQT[nxt]])
                    for i in range(nq):
                        for h2 in range(2):
                            P.op("pe", lambda e, i=i, h2=h2, pt=pt, Qn=Qn: e.matmul(
                                pt[:, h2, i, :], lhsT=Qn[:, h2, i, :], rhs=Tmb[:, h2, i, :], start=True, stop=True),
                                [RQ[nxt], RTmb], [Rb[2]])
                    dvo(lambda e, pt=pt: e.tensor_tensor(out=sq_(Tm32), in0=sq_(pt), in1=sq_(Tm32), op=ALU.add), [Rb[2], RTm32], [RTm32])
                    aco(lambda e: e.activation(out=sq_(Tmb), in_=sq_(Tm32), func=AF.Copy), [RTm32], [RTmb])
                    cur = nxt
                Xv = Xb.rearrange("p (h v) -> p h v", h=2)
                Uv = Ub.rearrange("p (h v) -> p h v", h=2)
                Xsv = Xs.rearrange("p (h v) -> p h v", h=2)
                px2 = PB(5)[0:64, 128:256].rearrange("p (h v) -> p h v", h=2)
                pu = PB(5)[0:64, 256:384].rearrange("p (h v) -> p h v", h=2)
                pst = PB(6)[:, 0:64]
                pxb = ps_t[0:64, 5:8:2, 0:64]
                for i in range(nq):
                    ci = q0 + i
                    for h2 in range(2):
                        hs = slice(64 * h2, 64 * h2 + 64)
                        bx = 5 + 2 * h2
                        P.op("pe", lambda e, hs=hs, h2=h2, ci=ci, bx=bx, c=c: e.matmul(
                            PB(bx)[0:64, 0:64], lhsT=arb[hs, ci, 0, :], rhs=STb[hs, c, :], start=True, stop=True),
                            [Rar, RSTb[c]], [Rb[bx]])
                        P.op("pe", lambda e, hs=hs, h2=h2, i=i: e.matmul(
                            px2[:, h2, :], lhsT=aktT[:, h2, i, :], rhs=tok[:, i, 2, hs], start=True, stop=True),
                            [RA1, Rtok], [Rb[5]])
                    aco(lambda e: e.activation(out=Xsv, in_=pxb, func=AF.Copy), [Rb[5], Rb[7]], [RXs])
                    dvo(lambda e: e.tensor_tensor(out=Xv, in0=px2, in1=Xsv, op=ALU.add), [Rb[5], RXs], [RXb])
                    for h2 in range(2):
                        P.op("pe", lambda e, h2=h2, i=i: e.matmul(
                            pu[:, h2, :], lhsT=Tmb[:, h2, i, :], rhs=Xv[:, h2, :], start=True, stop=True),
                            [RTmb, RXb], [Rb[5]])
                    aco(lambda e: e.activation(out=Uv, in_=pu, func=AF.Copy), [Rb[5]], [RUb])
                    for h2 in range(2):
                        hs = slice(64 * h2, 64 * h2 + 64)
                        bo = 4 if h2 == 0 else 7
                        po1 = PB(bo)[hs, 64 + i * 64:128 + i * 64]
                        po2 = PB(4)[hs, 320 + i * 32:320 + i * 32 + 32] if False else PB(3)[hs, i * 64:i * 64 + 64]
                        P.op("pe", lambda e, hs=hs, ci=ci, po1=po1, c=c: e.matmul(
                            po1, lhsT=STb[hs, c, :], rhs=arb[hs, ci, 1, :], start=True, stop=True),
                            [RSTb[c], Rar], [Rb[bo]])
                        P.op("pe", lambda e, hs=hs, h2=h2, i=i, po2=po2: e.matmul(
                            po2, lhsT=Uv[:, h2, :], rhs=arbT[:, h2, i, :], start=True, stop=False),
                            [RUb, RA1], [Rb[3]])
                        P.op("pe", lambda e, hs=hs, h2=h2, i=i, po2=po2: e.matmul(
                            po2, lhsT=tok[:, i, 2, hs], rhs=arkT[:, h2, i, :], start=False, stop=True),
                            [Rtok, RA1], [Rb[3]])
                    for h2 in range(2):
                        hs = slice(64 * h2, 64 * h2 + 64)
                        P.op("pe", lambda e, hs=hs, h2=h2, i=i: e.matmul(
                            pst[hs, :], lhsT=tok[:, i, 0, hs], rhs=Uv[:, h2, :], start=True, stop=False),
                            [Rtok, RUb], [Rb[6]])
                        P.op("pe", lambda e, hs=hs, h2=h2, i=i: e.matmul(
                            pst[hs, :], lhsT=tok[:, i, 1, hs], rhs=tok[:, i, 2, hs], start=False, stop=True),
                            [Rtok], [Rb[6]])
                    plc = W["pm"][:, ci * 64 + 63:ci * 64 + 64]
                    dvo(lambda e, plc=plc, c=c: e.scalar_tensor_tensor(out=ST32[:, c, :], in0=ST32[:, c, :], scalar=plc,
                                                                       in1=pst, op0=ALU.mult, op1=ALU.add),
                        [RST[c], R["pm"], Rb[6]], [RST[c]])
                    aco(lambda e, c=c: e.activation(out=STb[:, c, :], in_=ST32[:, c, :], func=AF.Copy), [RST[c]], [RSTb[c]])
                qs = slice(q0 * 64, (q0 + nq) * 64)
                aco(lambda e, qs=qs, nq=nq: e.activation(out=W["OT"][0:64, qs], in_=PB(4)[0:64, 64:64 + nq * 64], func=AF.Copy),
                    [Rb[4]], [R["OT"]])
                aco(lambda e, qs=qs, nq=nq: e.activation(out=W["OT"][64:128, qs], in_=PB(7)[64:128, 64:64 + nq * 64], func=AF.Copy),
                    [Rb[7], R["OT"]], [R["OT"]])
                dvo(lambda e, qs=qs, nq=nq: e.tensor_tensor(out=W["OT"][:, qs], in0=PB(3)[:, 0:nq * 64], in1=W["OT"][:, qs], op=ALU.add),
                    [Rb[3], R["OT"]], [R["OT"]])
            P.op("pe", lambda e: e.matmul(PB(2)[:, 0:n], lhsT=bones, rhs=W["OT"][:, 0:n], start=True, stop=True),
                 [Rcst, R["OT"]], [Rb[2]])
            dvo(lambda e: e.scalar_tensor_tensor(out=W["t1"][:, 0:n], in0=PB(2)[:, 0:n], scalar=-1.0 / 64, in1=W["OT"][:, 0:n],
                                                 op0=ALU.mult, op1=ALU.add), [Rb[2], R["OT"], R["t1"]], [R["t1"]])
            aco(lambda e: e.activation(out=W["t2"][:, 0:n], in_=W["t1"][:, 0:n], func=AF.Square), [R["t1"], R["t2"]], [R["t2"]])
            P.op("pe", lambda e: e.matmul(PB(3)[:, 0:n], lhsT=bones, rhs=W["t2"][:, 0:n], start=True, stop=True),
                 [Rcst, R["t2"]], [Rb[3]])
            aco(lambda e: e.activation(out=W["t2"][:, 0:n], in_=PB(3)[:, 0:n], func=AF.Sqrt, bias=epsT[:, 2:3],
                                       scale=1.0 / 64), [Rb[3], Reps, R["t2"]], [R["t2"]])
            dvo(lambda e: e.reciprocal(out=W["t2"][:, 0:n], in_=W["t2"][:, 0:n]), [R["t2"]], [R["t2"]])
            dvo(lambda e: e.tensor_tensor(out=W["t1"][:, 0:n], in0=W["t1"][:, 0:n], in1=W["t2"][:, 0:n], op=ALU.mult),
                [R["t1"], R["t2"]], [R["t1"]])
            dvo(lambda e, c=c: e.tensor_scalar(out=W["t1"][:, 0:n], in0=W["t1"][:, 0:n], scalar1=ppc("lw", c),
                                               scalar2=ppc("lb", c), op0=ALU.mult, op1=ALU.add), [R["t1"], Rpp], [R["t1"]])
            dvo(lambda e: e.tensor_tensor(out=W["t1"][:, 0:n], in0=W["t1"][:, 0:n], in1=W["bonus"][:, 0:n], op=ALU.add),
                [R["t1"], R["bonus"]], [R["t1"]])
            dvo(lambda e, c=c: e.tensor_tensor(out=ybT[:, c, 0:n], in0=W["t1"][:, 0:n], in1=W["gate"][:, 0:n], op=ALU.mult),
                [R["t1"], R["gate"]], [Ryb[c]])
            if DEBUG_TENSOR:
                dvo(lambda e, c=c: e.tensor_copy(out=ybT[:, c, 0:n], in_=W[DEBUG_TENSOR][:, 0:n]), [R[DEBUG_TENSOR]], [Ryb[c]])
        P.barrier()
        A.release(mr)

        if MIX_STOP in (2, 3):
            P.barrier(); A.release(mk); return
        ms = A.mark()
        u32 = A.alloc([8, TILE])
        ub = A.alloc([8, TILE], BF16)
        Ru = [P.res("u") for _ in range(8)]
        for c in range(8):
            bi = c % 2
            proj(COL_U + 128 * c, 128, n, PB(bi), Rb[bi])
            aco(lambda e, c=c, bi=bi: e.activation(out=u32[:, c, 0:n], in_=PB(bi)[:, 0:n], func=AF.Copy), [Rb[bi]], [Ru[c]])
            dvo(lambda e, c=c: e.tensor_copy(out=ub[:, c, 0:n], in_=u32[:, c, 0:n]), [Ru[c]], [Ru[c]])
        ya32 = A.alloc([8, TILE])
        Rya32 = P.res("ya32")
        HG = 8
        dpr = A.alloc([HG, 64])
        dpi = A.alloc([HG, 64])
        spr = A.alloc([HG, 64])
        spi = A.alloc([HG, 64])
        w1 = A.alloc([HG, 64])
        w2 = A.alloc([HG, 64])
        sreb = A.alloc([HG, 64], BF16)
        simb = A.alloc([HG, 64], BF16)
        cr = [A.alloc([HG]) for _ in range(4)]
        Rd, Rsp, Rw, Rsb, Rcr = P.res("dp"), P.res("sp_"), P.res("w12"), P.res("srb"), P.res("cr")
        for ci in range(nch):
            cc = slice(ci * 64, ci * 64 + 64)
            py = PB(4).rearrange("p (c t) -> p c t", t=64)
            for half in range(32 // HG):
                gs = slice(HG * half, HG * half + HG)
                for gl in range(HG):
                    gp = HG * half + gl
                    q4, cch, jj = gp % 4, gp // 4, gl // 4
                    rows = slice(32 * q4, 32 * q4 + 32) if q4 < 3 else slice(64, 128)
                    bb_ = bstb if q4 < 3 else bst3
                    P.op("pe", lambda e, rows=rows, cch=cch, cc=cc, bb_=bb_, q4=q4, jj=jj: e.matmul(
                        PB(q4)[:, jj * 64:jj * 64 + 64], lhsT=bb_[0][rows, cch, :], rhs=ub[rows, cch, cc], start=True, stop=True),
                        [Rbst[0], Ru[cch]], [Rb[q4]])
                    P.op("pe", lambda e, rows=rows, cch=cch, cc=cc, bb_=bb_, q4=q4, jj=jj: e.matmul(
                        PB(q4)[:, 128 + jj * 64:128 + jj * 64 + 64], lhsT=bb_[1][rows, cch, :], rhs=ub[rows, cch, cc], start=True,
                        stop=True), [Rbst[1], Ru[cch]], [Rb[q4]])
                rre = [Rb[0], Rb[1], Rb[2], Rb[3]]
                pre = ps_t[:, 0:4, 0:128].rearrange("p q (j t) -> p q j t", j=2)
                pim = ps_t[:, 0:4, 128:256].rearrange("p q (j t) -> p q j t", j=2)
                pv = lambda ap: ap.rearrange("p (q j) t -> p q j t", q=4)
                nv = lambda ap: ap.rearrange("p (j q) t -> p q j t", q=4)
                tc_, ts_ = nv(tcs[:, gs, :]), nv(tsn[:, gs, :])
                w1v, w2v, dprv, dpiv, sprv, spiv = [pv(x_) for x_ in (w1, w2, dpr, dpi, spr, spi)]
                dvo(lambda e, pre=pre, tc_=tc_: e.tensor_tensor(out=w1v, in0=pre, in1=tc_, op=ALU.mult), rre + [Rtab, Rw], [Rw])
                dvo(lambda e, pim=pim, ts_=ts_: e.tensor_tensor(out=w2v, in0=pim, in1=ts_, op=ALU.mult), rre + [Rtab, Rw], [Rw])
                dvo(lambda e: e.tensor_tensor(out=dpr, in0=w1, in1=w2, op=ALU.add), [Rw, Rd], [Rd])
                dvo(lambda e, pim=pim, tc_=tc_: e.tensor_tensor(out=w1v, in0=pim, in1=tc_, op=ALU.mult), rre + [Rtab, Rw], [Rw])
                dvo(lambda e, pre=pre, ts_=ts_: e.tensor_tensor(out=w2v, in0=pre, in1=ts_, op=ALU.mult), rre + [Rtab, Rw], [Rw])
                dvo(lambda e: e.tensor_tensor(out=dpi, in0=w1, in1=w2, op=ALU.subtract), [Rw, Rd], [Rd])
                nq_ = lambda ap: ap.rearrange("p (j q) -> p q j", q=4).unsqueeze(3)
                dvo(lambda e, gs=gs: e.tensor_tensor(out=cr[0], in0=rho[:, gs], in1=s5s[0][:, gs], op=ALU.mult), [Rtab, Rs5, Rcr], [Rcr])
                dvo(lambda e, gs=gs: e.tensor_tensor(out=cr[1], in0=rho[:, gs], in1=s5s[1][:, gs], op=ALU.mult), [Rtab, Rs5, Rcr], [Rcr])
                dvo(lambda e: e.tensor_tensor(out=dprv[:, :, :, 0:1], in0=dprv[:, :, :, 0:1], in1=nq_(cr[0]), op=ALU.add), [Rd, Rcr], [Rd])
                dvo(lambda e: e.tensor_tensor(out=dpiv[:, :, :, 0:1], in0=dpiv[:, :, :, 0:1], in1=nq_(cr[1]), op=ALU.add), [Rd, Rcr], [Rd])
                fl = lambda ap: ap.rearrange("p g t -> p (g t)")
                r0 = rho0[:, gs, :]
                dvo(lambda e, r0=r0: e.tensor_tensor_scan(out=fl(spr), data0=fl(r0), data1=fl(dpr), initial=0.0, op0=ALU.mult,
                                                          op1=ALU.add), [Rd, Rtab, Rsp], [Rsp])
                dvo(lambda e, r0=r0: e.tensor_tensor_scan(out=fl(spi), data0=fl(r0), data1=fl(dpi), initial=0.0, op0=ALU.mult,
                                                          op1=ALU.add), [Rd, Rtab, Rsp], [Rsp])
                dvo(lambda e, tc_=tc_: e.tensor_tensor(out=w1v, in0=sprv, in1=tc_, op=ALU.mult), [Rsp, Rtab, Rw], [Rw])
                dvo(lambda e, ts_=ts_: e.tensor_tensor(out=w2v, in0=spiv, in1=ts_, op=ALU.mult), [Rsp, Rtab, Rw], [Rw])
                dvo(lambda e: e.tensor_tensor(out=sreb, in0=w1, in1=w2, op=ALU.subtract), [Rw, Rsb], [Rsb])
                dvo(lambda e, gs=gs: e.tensor_tensor(out=nq_(s5s[0][:, gs]), in0=w1v[:, :, :, 63:64], in1=w2v[:, :, :, 63:64],
                                                     op=ALU.subtract), [Rw, Rs5], [Rs5])
                dvo(lambda e, ts_=ts_: e.tensor_tensor(out=w1v, in0=sprv, in1=ts_, op=ALU.mult), [Rsp, Rtab, Rw], [Rw])
                dvo(lambda e, tc_=tc_: e.tensor_tensor(out=w2v, in0=spiv, in1=tc_, op=ALU.mult), [Rsp, Rtab, Rw], [Rw])
                dvo(lambda e: e.tensor_tensor(out=simb, in0=w1, in1=w2, op=ALU.add), [Rw, Rsb], [Rsb])
                dvo(lambda e, gs=gs: e.tensor_tensor(out=nq_(s5s[1][:, gs]), in0=w1v[:, :, :, 63:64], in1=w2v[:, :, :, 63:64],
                                                     op=ALU.add), [Rw, Rs5], [Rs5])
                for gl in range(HG):
                    gp = HG * half + gl
                    q4, cch = gp % 4, gp // 4
                    lo = (gl % 4) * 2 + gl // 4
                    rows = slice(64 * (q4 // 2), 64 * (q4 // 2) + 64)
                    P.op("pe", lambda e, gp=gp, lo=lo, rows=rows, cch=cch, py=py, q4=q4: e.matmul(
                        py[rows, cch, :], lhsT=cqb[0][:, gp, :], rhs=sreb[:, lo, :], start=(q4 % 2 == 0), stop=False),
                        [Rcq, Rsb], [Rb[4]])
                    P.op("pe", lambda e, gp=gp, lo=lo, rows=rows, cch=cch, py=py, q4=q4: e.matmul(
                        py[rows, cch, :], lhsT=cqb[1][:, gp, :], rhs=simb[:, lo, :], start=False, stop=(q4 % 2 == 1)),
                        [Rcq, Rsb], [Rb[4]])
            sdb = ppw("sd").unsqueeze(2).to_broadcast([128, 8, 64])
            dvo(lambda e, cc=cc, sdb=sdb: e.tensor_tensor(out=ya32[:, :, cc], in0=u32[:, :, cc], in1=sdb, op=ALU.mult),
                Ru + [Rpp, Rya32], [Rya32])
            dvo(lambda e, cc=cc, py=py: e.tensor_tensor(out=ya32[:, :, cc], in0=py, in1=ya32[:, :, cc], op=ALU.add),
                [Rb[4], Rya32], [Rya32])
        ygb = ub
        Ryg = P.res("yg")
        gt = u32
        for c in range(8):
            dvo(lambda e, c=c: e.tensor_tensor(out=gt[:, c, 0:n], in0=ya32[:, c, 0:n], in1=ya32[:, c, 0:n], op=ALU.mult),
                [Rya32] + Ru, [Ru[c]])
            dvo(lambda e, c=c: e.tensor_scalar(out=gt[:, c, 0:n], in0=gt[:, c, 0:n], scalar1=0.044715, scalar2=1.0,
                                               op0=ALU.mult, op1=ALU.add), [Ru[c]], [Ru[c]])
            dvo(lambda e, c=c: e.tensor_tensor(out=gt[:, c, 0:n], in0=gt[:, c, 0:n], in1=ya32[:, c, 0:n], op=ALU.mult),
                [Ru[c], Rya32], [Ru[c]])
            aco(lambda e, c=c: e.activation(out=gt[:, c, 0:n], in_=gt[:, c, 0:n], func=AF.Sigmoid, scale=GELU_K), [Ru[c]], [Ru[c]])
            dvo(lambda e, c=c: e.tensor_tensor(out=ya32[:, c, 0:n], in0=ya32[:, c, 0:n], in1=gt[:, c, 0:n], op=ALU.mult),
                [Ru[c], Rya32], [Rya32])
            aco(lambda e, c=c: e.activation(out=ygb[:, c, 0:n], in_=ya32[:, c, 0:n], func=AF.Copy), [Rya32, Ryg], [Ryg])
        P.barrier()
        wg_b = [x_.rearrange("p a b -> p (a b)").bitcast(BF16).rearrange("p (a b) -> p a b", a=8) for x_ in (dpr, dpi)]
        Rwgl = [P.pres("wglu%d" % i_) for i_ in range(2)]
        wgluv = wglu.rearrange("(k p) c -> p k c", p=128)
        for c in range(8):
            b = c % 2
            P.dma("pool", lambda e, b=b, c=c: e.dma_start(out=wg_b[b], in_=wgluv[:, :, 128 * c:128 * c + 128]), Rwgl[b])
            for k in range(8):
                P.op("pe", lambda e, b=b, k=k: e.matmul(PB(5 + b)[:, 0:n], lhsT=wg_b[b][:, k, :], rhs=ygb[:, k, 0:n],
                                                        start=(k == 0), stop=(k == 7)), [Rwgl[b], Ryg], [Rb[5 + b]])
            aco(lambda e, c=c, b=b: e.activation(out=gt[:, c, 0:n], in_=PB(5 + b)[:, 0:n], func=AF.Sigmoid), [Rb[5 + b], Ru[c]], [Ru[c]])
            dvo(lambda e, c=c: e.tensor_tensor(out=yaT[:, c, 0:n], in0=ya32[:, c, 0:n], in1=gt[:, c, 0:n], op=ALU.mult),
                [Rya32, Ru[c]], [Rya[c]])
        P.barrier()
        A.release(ms)

        if g["dbg"] is not None:
            dd = g["dbg"]
            Rdbg = g["Rdbg"]
            mdb = A.mark()
            dbt = A.alloc([16, n])
            Rdbt = P.res("dbt")
            for c in range(8):
                P.op("dve", lambda e, c=c: e.tensor_copy(out=dbt[:, c, 0:n], in_=yaT[:, c, 0:n]), [Rya[c], Rdbt], [Rdbt])
                P.op("dve", lambda e, c=c: e.tensor_copy(out=dbt[:, 8 + c, 0:n], in_=ybT[:, c, 0:n]), [Ryb[c], Rdbt], [Rdbt])
            P.dma("sp", lambda e: e.dma_start(out=dd[:, :, t0:t0 + n].rearrange("j p t -> p j t"), in_=dbt[:, :, 0:n]), Rdbg,
                  reads=[Rdbt])
            P.barrier()
            A.release(mdb)
        mT = A.alloc([NDC, TILE], BF16)
        RmT = P.res("mT")
        wu_b = [A.alloc([8, 128], BF16) for _ in range(4)]
        Rwu_ = [P.pres("wup%d" % i_) for i_ in range(4)]
        gaT = [A.alloc([TILE]) for _ in range(2)]
        Rga = [P.res("ga") for _ in range(2)]
        wupsv = wups.rearrange("(k p) c -> p k c", p=128)
        wuprv = wupr.rearrange("(k p) c -> p k c", p=128)
        for d in range(NDC):
            b = d % 2
            dc = slice(128 * d, 128 * d + 128)
            P.dma("pool", lambda e, b=b, dc=dc: e.dma_start(out=wu_b[b], in_=wupsv[:, :, dc]), Rwu_[b])
            P.dma("pool", lambda e, b=b, dc=dc: e.dma_start(out=wu_b[2 + b], in_=wuprv[:, :, dc]), Rwu_[2 + b])
            proj(COL_GA + 128 * d, 128, n, PB(0), Rb[0])
            aco(lambda e: e.activation(out=gaT[0][:, 0:n], in_=PB(0)[:, 0:n], func=AF.Sigmoid), [Rb[0]], [Rga[0]])
            proj(COL_GB + 128 * d, 128, n, PB(1), Rb[1])
            aco(lambda e: e.activation(out=gaT[1][:, 0:n], in_=PB(1)[:, 0:n], func=AF.Sigmoid), [Rb[1]], [Rga[1]])
            for k in range(8):
                P.op("pe", lambda e, b=b, k=k: e.matmul(PB(2)[:, 0:n], lhsT=wu_b[b][:, k, :], rhs=yaT[:, k, 0:n],
                                                        start=(k == 0), stop=(k == 7)), [Rwu_[b], Rya[k]], [Rb[2]])
            for k in range(8):
                P.op("pe", lambda e, b=b, k=k: e.matmul(PB(3)[:, 0:n], lhsT=wu_b[2 + b][:, k, :], rhs=ybT[:, k, 0:n],
                                                        start=(k == 0), stop=(k == 7)), [Rwu_[2 + b], Ryb[k]], [Rb[3]])
            dvo(lambda e: e.tensor_tensor(out=gaT[0][:, 0:n], in0=PB(2)[:, 0:n], in1=gaT[0][:, 0:n], op=ALU.mult),
                [Rb[2], Rga[0]], [Rga[0]])
            dvo(lambda e: e.tensor_tensor(out=gaT[1][:, 0:n], in0=PB(3)[:, 0:n], in1=gaT[1][:, 0:n], op=ALU.mult),
                [Rb[3], Rga[1]], [Rga[1]])
            dvo(lambda e, d=d: e.tensor_tensor(out=mT[:, d, 0:n], in0=gaT[0][:, 0:n], in1=gaT[1][:, 0:n], op=ALU.add),
                [Rga[0], Rga[1]], [RmT])
        woutv = wout.rearrange("(j p) c -> p j c", p=128)
        for d in range(NDC):
            b = wbi[0] % 3
            wbi[0] += 1
            dc = slice(128 * d, 128 * d + 128)
            bi = 4 + d % 2
            wb_ = wbuf[b]
            P.dma("pool", lambda e, wb_=wb_, dc=dc: e.dma_start(out=wb_, in_=woutv[:, :, dc]), Rwb[b])
            for j in range(NDC):
                P.op("pe", lambda e, wb_=wb_, j=j, bi=bi: e.matmul(PB(bi)[:, 0:n], lhsT=wb_[:, j, :], rhs=mT[:, j, 0:n],
                                                                   start=(j == 0), stop=(j == NDC - 1)), [Rwb[b], RmT], [Rb[bi]])
            dvo(lambda e, d=d, bi=bi: e.tensor_tensor(out=X[:, d, 0:n], in0=PB(bi)[:, 0:n], in1=X[:, d, 0:n], op=ALU.add),
                [Rb[bi], RX[d]], [RX[d]])
        P.barrier()
        A.release(mk)

    return mixer


def make_consts():
    c = np.zeros((128, NCST), np.float32)
    o = CS["ones"][0]
    c[:, o:o + 128] = 1.0
    o = CS["bones"][0]
    c[0:64, o:o + 64] = 1.0
    c[64:128, o + 64:o + 128] = 1.0
    o = CS["ident"][0]
    c[:, o:o + 128] = np.eye(128, dtype=np.float32)
    i = np.arange(64)[:, None]
    t = np.arange(64)[None, :]
    su = (i < t).astype(np.float32)
    iu = (i <= t).astype(np.float32)
    m1 = np.zeros((64, 2, 2, 2, 64), np.float32)
    m1[:, :, :, 0, :] = su[:, None, None, :]
    m1[:, :, :, 1, :] = iu[:, None, None, :]
    o = CS["m1"][0]
    c[0:64, o:o + 512] = m1.reshape(64, 512)
    sl = (i > t).astype(np.float32)
    o = CS["msl"][0]
    c[0:64, o:o + 128] = np.concatenate([sl, sl], 1)
    o = CS["ifree"][0]
    c[0:64, o:o + 128] = np.concatenate([np.eye(64), np.eye(64)], 1)
    o = CS["rmask"][0]
    rm = np.ones(512, np.float32)
    rm[0::64] = 0.0
    c[:, o:o + 512] = rm[None, :]
    return c


def pcol(v):
    v = np.asarray(v, np.float32)
    return np.ascontiguousarray(v.reshape(-1, 128).T)


def layer_inputs(inp, i, consts):
    f32 = np.float32
    p = np.zeros((128, NPP), f32)

    def put(name, arr):
        o, w = PP[name]
        arr = np.asarray(arr, f32)
        p[0:arr.shape[0], o:o + arr.shape[1]] = arr

    put("g1", pcol(inp["ffn1_norm"][i]))
    put("gm", pcol(inp["mix_norm"][i]))
    put("g2", pcol(inp["ffn2_norm"][i]))
    put("gf", pcol(inp["final_norm"]))
    mu = np.asarray(inp["mu_shift"][i], f32)
    put("mur", pcol(mu[0:1024]))
    put("muk", pcol(mu[1024:2048]))
    put("muv", pcol(mu[2048:3072]))
    put("muwa", mu[3072:3200].reshape(128, 1))
    put("mug1", mu[3200:3328].reshape(128, 1))
    put("mug2", mu[3328:3360].reshape(32, 1))
    if i > 0:
        put("muvr", np.asarray(inp["mu_vres"][i - 1], f32).reshape(32, 1))
        put("v0", pcol(inp["rwkv_v0"][i - 1]))
        put("flag", np.ones((128, 1), f32))
    put("w0", pcol(inp["rwkv_w0"][i]))
    put("a0", pcol(inp["rwkv_a0"][i]))
    put("kk", pcol(inp["rwkv_k_k"][i]))
    put("ka", pcol(inp["rwkv_k_a"][i]))
    put("rk", pcol(np.asarray(inp["rwkv_r_k"][i]).reshape(-1)))
    put("lw", pcol(inp["rwkv_lnx_w"][i]))
    put("lb", pcol(inp["rwkv_lnx_b"][i]))
    put("sd", pcol(inp["ssm_d"][i]))
    put("lre", pcol(np.asarray(inp["ssm_lambda_re"][i]).reshape(-1)))
    put("lim", pcol(np.asarray(inp["ssm_lambda_im"][i]).reshape(-1)))
    ldt = np.repeat(np.asarray(inp["ssm_log_dt"][i], f32), 64)
    put("ldt", pcol(ldt))

    if i == 0:
        w_in = np.zeros((D, PIN), f32)
        w_in[:, :8480] = inp["w_in_first"]
        v2 = np.zeros((32, 1024), f32)
    else:
        w_in = np.ascontiguousarray(inp["w_in_rest"][i - 1], dtype=f32)
        v2 = np.ascontiguousarray(inp["rwkv_v2"][i - 1], dtype=f32)
    wa2 = np.concatenate([inp["rwkv_w2"][i], inp["rwkv_a2"][i]], 0).astype(f32)
    g2 = np.asarray(inp["rwkv_g2"][i], f32)
    def bst(b):
        b = np.asarray(b, f32)
        out = np.zeros((4, 2, 16, 8, 2, 64), f32)
        bb = b.reshape(8, 4, 2, 64, 16)
        for g2_ in range(2):
            out[:, g2_, :, :, g2_, :] = bb[:, :, g2_, :, :].transpose(1, 3, 0, 2)
        return np.ascontiguousarray(out.reshape(128, 8 * 128))

    def bst3(b):
        o = bst(b).copy()
        o[0:96] = 0.0
        return o

    def cst_(cm):
        cm = np.asarray(cm, f32)
        out = np.zeros((2, 64, 32, 2, 2, 16), f32)
        cc = cm.reshape(32, 2, 16, 64)
        for g2_ in range(2):
            for hf in range(2):
                out[g2_, :, hf::2, hf, g2_, :] = cc[hf::2, g2_, :, :].transpose(2, 0, 1)
        return np.ascontiguousarray(out.reshape(128, 32 * 64))

    return {
        "pp": p, "cst": consts,
        "wg1": np.ascontiguousarray(inp["ffn1_w_gate"][i], dtype=f32),
        "wu1": np.ascontiguousarray(inp["ffn1_w_up"][i], dtype=f32),
        "wd1": np.ascontiguousarray(inp["ffn1_w_down"][i], dtype=f32),
        "wg2": np.ascontiguousarray(inp["ffn2_w_gate"][i], dtype=f32),
        "wu2": np.ascontiguousarray(inp["ffn2_w_up"][i], dtype=f32),
        "wd2": np.ascontiguousarray(inp["ffn2_w_down"][i], dtype=f32),
        "win": w_in,
        "wglu": np.ascontiguousarray(inp["ssm_w_glu"][i], dtype=f32),
        "wups": np.ascontiguousarray(inp["w_up_ssm"][i], dtype=f32),
        "wupr": np.ascontiguousarray(inp["w_up_rwkv"][i], dtype=f32),
        "wout": np.ascontiguousarray(inp["w_out"][i], dtype=f32),
        "wa2": wa2, "g2a": np.ascontiguousarray(g2[0:128]), "g2b": np.ascontiguousarray(g2[128:160]), "v2": v2,
        "bre": bst(inp["ssm_b_re"][i]), "bim": bst(inp["ssm_b_im"][i]),
        "bre3": bst3(inp["ssm_b_re"][i]), "bim3": bst3(inp["ssm_b_im"][i]),
        "cre": cst_(inp["ssm_c_re"][i]), "cim": cst_(inp["ssm_c_im"][i]),
    }


def to_fm(a, nchunks):
    T = a.shape[0]
    return np.ascontiguousarray(a.reshape(T, nchunks, 128).transpose(1, 2, 0))


_PROG = {}


def kernel(**inp):
    inp = {k: np.asarray(v) for k, v in inp.items()}
    x = inp["x"].astype(np.float32)
    B = x.shape[0]
    depth = inp["ffn1_norm"].shape[0]
    if "nc" not in _PROG:
        _PROG["nc"] = build_program()
    nc = _PROG["nc"]
    consts = make_consts()
    meta = inp["meta_tokens"].astype(np.float32)
    xs = []
    for b in range(B):
        h = np.zeros((TPAD, D), np.float32)
        h[0:NMETA] = meta
        h[NMETA:NMETA + SEQ] = x[b]
        xs.append(to_fm(h, NDC))
    vfs = [np.zeros((8, 128, TPAD), np.float32) for _ in range(B)]
    z = None
    for i in range(depth):
        li = layer_inputs(inp, i, consts)
        in_maps = []
        for b in range(B):
            m = dict(li)
            m["xT"] = xs[b]
            m["vfT"] = vfs[b]
            in_maps.append(m)
        res = run_bass_kernel_spmd(nc, in_maps, core_ids=list(range(B)))
        xs = [r["yT"] for r in res.results]
        if i == 0:
            vfs = [r["vfo"] for r in res.results]
        z = [r["zT"] for r in res.results]
    out = np.stack([zb.transpose(2, 0, 1).reshape(TPAD, D)[NMETA:NMETA + SEQ] for zb in z], 0)
    return out.astype(np.float32)
```

```python
import contextlib
import math
import numpy as np
import ml_dtypes
import concourse.bass as bass
import concourse.mybir as mybir
from concourse.bass_utils import run_bass_kernel_spmd

F32 = mybir.dt.float32
BF16 = mybir.dt.bfloat16
ALU = mybir.AluOpType
AF = mybir.ActivationFunctionType

D = 2048
DFF = 5632
NFF = DFF // 128
NDC = D // 128
SEQ = 4096
NMETA = 16
TPAD = 4160
PIN = 8512
C0 = math.exp(-0.5)
NORM_EPS = 1e-6
LNX_EPS = 64 * 1e-5
GELU_K = 2.0 * math.sqrt(2.0 / math.pi)
DEBUG_TENSOR = None
MIX_STOP = 0

COL_U, COL_R, COL_K, COL_V, COL_WA, COL_G, COL_GA, COL_GB, COL_VR = 0, 1024, 2048, 3072, 4096, 4224, 4384, 6432, 8480

PP = {}
_o = 0
for _n, _w in (("g1", 16), ("gm", 16), ("g2", 16), ("gf", 16), ("mur", 8), ("muk", 8), ("muv", 8),
               ("muwa", 1), ("mug1", 1), ("mug2", 1), ("muvr", 1), ("w0", 8), ("a0", 8), ("v0", 8),
               ("kk", 8), ("ka", 8), ("rk", 8), ("lw", 8), ("lb", 8), ("sd", 8),
               ("lre", 32), ("lim", 32), ("ldt", 32), ("flag", 1)):
    PP[_n] = (_o, _w)
    _o += _w
NPP = _o

CS = {}
_o = 0
for _n, _w in (("ones", 128), ("bones", 128), ("ident", 128), ("m1", 512), ("msl", 128), ("ifree", 128),
               ("rmask", 512)):
    CS[_n] = (_o, _w)
    _o += _w
NCST = _o


class Res:
    __slots__ = ("name", "w", "r", "dsem", "dcnt")

    def __init__(self, name):
        self.name = name
        self.w = None
        self.r = []
        self.dsem = None
        self.dcnt = 0


class Prog:
    ENG = ("pe", "dve", "act", "pool", "sp")

    def __init__(self, nc):
        self.nc = nc
        self.ops = {e: [] for e in self.ENG}
        self.cnt = {e: 0 for e in self.ENG}
        self.seen = {e: {} for e in self.ENG}
        self.semkeys = list(self.ENG)
        self.dma_res = []
        self.n_instr = 0
        self._rid = 0

    def res(self, name="r"):
        self._rid += 1
        return Res("%s%d" % (name, self._rid))

    def pres(self, key):
        if not hasattr(self, "_pres"):
            self._pres = {}
        if key not in self._pres:
            self._pres[key] = self.res(key)
        return self._pres[key]

    def _deps(self, eng, reads, writes, noself):
        ev = {}

        def add(e):
            if e is None:
                return
            k, v = e
            if noself and k == eng:
                return
            if ev.get(k, 0) < v:
                ev[k] = v
        for r in reads:
            add(r.w)
        for w in writes:
            add(w.w)
            for e in w.r:
                add(e)
        waits = []
        seen = self.seen[eng]
        for k, v in ev.items():
            if seen.get(k, 0) < v:
                seen[k] = v
                waits.append((k, v))
        return waits

    def _commit(self, event, reads, writes):
        for r in reads:
            r.r.append(event)
            if len(r.r) > 32:
                d = {}
                for k, v in r.r:
                    if d.get(k, 0) < v:
                        d[k] = v
                r.r = list(d.items())
        for w in writes:
            w.w = event
            w.r = []

    def op(self, eng, fn, reads=(), writes=(), noself=None):
        if noself is None:
            noself = (eng == "pe")
        waits = self._deps(eng, reads, writes, noself)
        self.cnt[eng] += 1
        event = (eng, self.cnt[eng])
        self.ops[eng].append((fn, waits, (eng, 1)))
        self._commit(event, reads, writes)
        self.n_instr += 1
        return event

    def dma(self, eng, fn, dst, reads=(), also=()):
        if dst.dsem is None:
            dst.dsem = "d_" + dst.name
            self.semkeys.append(dst.dsem)
            self.dma_res.append(dst)
        writes = (dst,) + tuple(also)
        waits = self._deps(eng, reads, writes, False)
        dst.dcnt += 16
        event = (dst.dsem, dst.dcnt)
        self.ops[eng].append((fn, waits, (dst.dsem, 16)))
        self._commit(event, reads, writes)
        self.n_instr += 1
        return event

    def barrier(self):
        ev = [(e, self.cnt[e]) for e in self.ENG if self.cnt[e] > 0]
        ev += [(r.dsem, r.dcnt) for r in self.dma_res]
        for eng in self.ENG:
            waits = []
            seen = self.seen[eng]
            for k, v in ev:
                if seen.get(k, 0) < v:
                    seen[k] = v
                    waits.append((k, v))
            if waits:
                self.ops[eng].append((None, waits, None))

    def wait_all(self, eng, resources):
        waits = self._deps(eng, resources, (), False)
        self.ops[eng].append((None, waits, None))

    def emit(self):
        nc = self.nc
        with contextlib.ExitStack() as st:
            sems = {}
            for k in self.semkeys:
                sems[k] = st.enter_context(nc.semaphore("s_" + k))
            block = st.enter_context(nc.Block())

            def run(engname):
                def body(e):
                    for fn, waits, inc in self.ops[engname]:
                        for k, v in waits:
                            e.wait_ge(sems[k], v)
                        if fn is not None:
                            fn(e).then_inc(sems[inc[0]], inc[1])
                return body

            block.tensor(run("pe"))
            block.vector(run("dve"))
            block.scalar(run("act"))
            block.gpsimd(run("pool"))
            block.sync(run("sp"))


class Arena:
    def __init__(self, ap, words):
        self.ap = ap
        self.words = words
        self.off = 0
        self.peak = 0

    def mark(self):
        return self.off

    def release(self, m):
        self.off = m

    def alloc(self, shape, dtype=F32, parts=128):
        free = 1
        for s in shape:
            free *= s
        words = free if dtype == F32 else (free + 1) // 2
        assert self.off + words <= self.words, ("SBUF arena overflow", self.off, words, self.words)
        a = self.ap[0:parts, self.off:self.off + words]
        self.off += words
        self.peak = max(self.peak, self.off)
        if dtype != F32:
            a = a.bitcast(dtype)
        if len(shape) == 2:
            return a.rearrange("p (a b) -> p a b", a=shape[0])
        if len(shape) == 3:
            return a.rearrange("p (a b c) -> p a b c", a=shape[0], b=shape[1])
        if len(shape) == 4:
            return a.rearrange("p (a b c d) -> p a b c d", a=shape[0], b=shape[1], c=shape[2])
        return a


def build_program(TP=TPAD, TILE=512, stages=("ffn1", "mix", "ffn2"), debug=False):
    nc = bass.Bass("TRN2", target_bir_lowering=False)
    dt = nc.dram_tensor

    def din(name, shape):
        return dt(name, shape, F32, kind="ExternalInput").ap()

    xT = din("xT", [NDC, 128, TP])
    vfT = din("vfT", [8, 128, TP])
    pp_d = din("pp", [128, NPP])
    cst_d = din("cst", [128, NCST])
    wg1, wu1, wd1 = din("wg1", [D, DFF]), din("wu1", [D, DFF]), din("wd1", [DFF, D])
    wg2, wu2, wd2 = din("wg2", [D, DFF]), din("wu2", [D, DFF]), din("wd2", [DFF, D])
    win = din("win", [D, PIN])
    wglu = din("wglu", [1024, 1024])
    wups = din("wups", [1024, D])
    wupr = din("wupr", [1024, D])
    wout = din("wout", [D, D])
    wa2 = din("wa2", [128, 1024])
    g2a = din("g2a", [128, 1024])
    g2b = din("g2b", [32, 1024])
    v2d = din("v2", [32, 1024])
    bre_d = din("bre", [128, 8 * 128])
    bim_d = din("bim", [128, 8 * 128])
    bre3_d = din("bre3", [128, 8 * 128])
    bim3_d = din("bim3", [128, 8 * 128])
    cre_d = din("cre", [128, 32 * 64])
    cim_d = din("cim", [128, 32 * 64])
    yT = dt("yT", [NDC, 128, TP], F32, kind="ExternalOutput").ap()
    zT = dt("zT", [NDC, 128, TP], F32, kind="ExternalOutput").ap()
    vfo = dt("vfo", [8, 128, TP], F32, kind="ExternalOutput").ap()
    dbg = dt("dbg", [16, 128, TP], F32, kind="ExternalOutput").ap() if debug else None

    tiles = []
    t0 = 0
    while t0 < TP:
        n = min(TILE, TP - t0)
        tiles.append((t0, n))
        t0 += n

    st = contextlib.ExitStack()
    AW = 53100
    arena_t = st.enter_context(nc.sbuf_tensor("arena", [128, AW], F32))
    ps_t = st.enter_context(nc.psum_tensor("ps", [128, 8, 512], F32))
    A = Arena(arena_t, AW)
    P = Prog(nc)

    Rb = [P.res("bank") for _ in range(8)]
    bank_i = [0]

    def bank():
        i = bank_i[0] % 8
        bank_i[0] += 1
        return ps_t[:, i, :], Rb[i]

    pp = A.alloc([NPP])
    cst = A.alloc([NCST])
    Rpp, Rcst = P.res("pp"), P.res("cst")
    P.dma("sp", lambda e: e.dma_start(out=pp, in_=pp_d[:, :]), Rpp)
    P.dma("sp", lambda e: e.dma_start(out=cst, in_=cst_d[:, :]), Rcst)

    def ppc(name, j=0, parts=128, p0=0):
        o, w = PP[name]
        return pp[p0:p0 + parts, o + j:o + j + 1]

    def ppw(name):
        o, w = PP[name]
        return pp[:, o:o + w]

    def csl(name, parts=128):
        o, w = CS[name]
        return cst[0:parts, o:o + w]

    ones = csl("ones")
    bones = csl("bones")
    identb = A.alloc([128], BF16)
    Rid = P.res("ident")
    P.op("dve", lambda e: e.tensor_copy(out=identb, in_=csl("ident")), [Rcst], [Rid])
    epsT = A.alloc([4])
    Reps = P.res("eps")
    for k_, v_ in enumerate((NORM_EPS, 1e-12, LNX_EPS, 0.0)):
        P.op("dve", lambda e, k_=k_, v_=v_: e.memset(epsT[:, k_:k_ + 1], v_), [], [Reps])
    omka = A.alloc([8])
    Romka = P.res("omka")
    P.op("dve", lambda e: e.tensor_scalar(out=omka, in0=ppw("ka"), scalar1=-1.0, scalar2=1.0, op0=ALU.mult,
                                          op1=ALU.add), [Rpp], [Romka])

    X = A.alloc([NDC, TILE])
    RX = [P.res("X") for _ in range(NDC)]
    hT = A.alloc([NDC, TILE], BF16)
    RhT = P.res("hT")
    rstd = A.alloc([TILE])
    Rrstd = P.res("rstd")
    sqt = [A.alloc([TILE]) for _ in range(2)]
    Rsq = [P.res("sq") for _ in range(2)]

    def dbg_out(idx, ap, res, t0, n, parts=128):
        if dbg is not None:
            P.dma("sp", lambda e: e.dma_start(out=dbg[idx, 0:parts, t0:t0 + n], in_=ap), Rdbg, reads=[res])

    Rdbg = P.res("dbg")

    def rmsnorm_stats(n):
        bk, rb = bank()
        for j in range(NDC):
            s = sqt[j % 2]
            P.op("act", lambda e, j=j, s=s: e.activation(out=s[:, 0:n], in_=X[:, j, 0:n], func=AF.Square),
                 [RX[j]], [Rsq[j % 2]])
            P.op("pe", lambda e, j=j, s=s: e.matmul(bk[:, 0:n], lhsT=ones, rhs=s[:, 0:n], start=(j == 0),
                                                     stop=(j == NDC - 1)), [Rsq[j % 2], Rcst], [rb])
        P.op("act", lambda e: e.activation(out=rstd[:, 0:n], in_=bk[:, 0:n], func=AF.Sqrt, bias=epsT[:, 0:1],
                                           scale=1.0 / D), [rb, Reps], [Rrstd])
        P.op("dve", lambda e: e.reciprocal(out=rstd[:, 0:n], in_=rstd[:, 0:n]), [Rrstd], [Rrstd])

    def norm_apply(gname, n, out_fn, out_res_fn):
        for j in range(NDC):
            P.op("dve", lambda e, j=j: e.scalar_tensor_tensor(out=out_fn(j), in0=X[:, j, 0:n], scalar=ppc(gname, j),
                                                              in1=rstd[:, 0:n], op0=ALU.mult, op1=ALU.mult),
                 [RX[j], Rrstd, Rpp], [out_res_fn(j)])

    def ffn(wg, wu, wd, n):
        m = A.mark()
        actT = A.alloc([NFF, TILE], BF16)
        Ract = [P.res("act") for _ in range(NFF)]
        HK = NDC // 2
        wgb = [A.alloc([HK, 256], BF16) for _ in range(2)]
        wub = [A.alloc([HK, 256], BF16) for _ in range(2)]
        Rwg = [P.pres("wg%d" % i_) for i_ in range(2)]
        Rwu = [P.pres("wu%d" % i_) for i_ in range(2)]
        HF = NFF // 2
        wdb = [A.alloc([HF, 256], BF16) for _ in range(2)]
        Rwd = [P.pres("wd%d" % i_) for i_ in range(2)]
        stmp = [A.alloc([TILE]) for _ in range(2)]
        Rst = [P.res("stmp") for _ in range(2)]
        wgv = wg.rearrange("(j p) c -> p j c", p=128)
        wuv = wu.rearrange("(j p) c -> p j c", p=128)
        wdv = wd.rearrange("(f p) c -> p f c", p=128)
        ld = [0]
        for g in range(NFF // 2):
            c0 = g * 256
            banks_g = [bank() for _ in range(2)]
            banks_u = [bank() for _ in range(2)]
            for kh in range(2):
                b = ld[0] % 2
                ld[0] += 1
                js = slice(kh * HK, kh * HK + HK)
                P.dma("pool", lambda e, b=b, c0=c0, js=js: e.dma_start(out=wgb[b], in_=wgv[:, js, c0:c0 + 256]), Rwg[b])
                P.dma("pool", lambda e, b=b, c0=c0, js=js: e.dma_start(out=wub[b], in_=wuv[:, js, c0:c0 + 256]), Rwu[b])
                for fl in range(2):
                    bg, rg = banks_g[fl]
                    for jj in range(HK):
                        j = kh * HK + jj
                        P.op("pe", lambda e, jj=jj, j=j, b=b, fl=fl, bg=bg: e.matmul(
                            bg[:, 0:n], lhsT=wgb[b][:, jj, fl * 128:(fl + 1) * 128], rhs=hT[:, j, 0:n],
                            start=(j == 0), stop=(j == NDC - 1)), [Rwg[b], RhT], [rg])
                for fl in range(2):
                    bu, ru = banks_u[fl]
                    for jj in range(HK):
                        j = kh * HK + jj
                        P.op("pe", lambda e, jj=jj, j=j, b=b, fl=fl, bu=bu: e.matmul(
                            bu[:, 0:n], lhsT=wub[b][:, jj, fl * 128:(fl + 1) * 128], rhs=hT[:, j, 0:n],
                            start=(j == 0), stop=(j == NDC - 1)), [Rwu[b], RhT], [ru])
            for fl in range(2):
                f = g * 2 + fl
                bg, rg = banks_g[fl]
                bu, ru = banks_u[fl]
                s = stmp[f % 2]
                P.op("act", lambda e, s=s, bg=bg: e.activation(out=s[:, 0:n], in_=bg[:, 0:n], func=AF.Silu),
                     [rg], [Rst[f % 2]])
                P.op("dve", lambda e, s=s, bu=bu, f=f: e.tensor_tensor(out=actT[:, f, 0:n], in0=bu[:, 0:n],
                                                                       in1=s[:, 0:n], op=ALU.mult),
                     [ru, Rst[f % 2]], [Ract[f]])
        for dg in range(NDC // 2):
            banks_d = [bank() for _ in range(2)]
            for fh in range(2):
                b = ld[0] % 2
                ld[0] += 1
                fs = slice(fh * HF, fh * HF + HF)
                P.dma("pool", lambda e, b=b, dg=dg, fs=fs: e.dma_start(out=wdb[b], in_=wdv[:, fs, dg * 256:(dg + 1) * 256]),
                      Rwd[b])
                for dl in range(2):
                    bk, rb = banks_d[dl]
                    for ff in range(HF):
                        f = fh * HF + ff
                        P.op("pe", lambda e, ff=ff, f=f, b=b, dl=dl, bk=bk: e.matmul(
                            bk[:, 0:n], lhsT=wdb[b][:, ff, dl * 128:(dl + 1) * 128], rhs=actT[:, f, 0:n],
                            start=(f == 0), stop=(f == NFF - 1)), [Rwd[b], Ract[f]], [rb])
            for dl in range(2):
                d = dg * 2 + dl
                bk, rb = banks_d[dl]
                P.op("dve", lambda e, d=d, bk=bk: e.scalar_tensor_tensor(out=X[:, d, 0:n], in0=bk[:, 0:n], scalar=0.5,
                                                                         in1=X[:, d, 0:n], op0=ALU.mult, op1=ALU.add),
                     [rb, RX[d]], [RX[d]])
        P.barrier()
        A.release(m)

    Ry, Rz, Rvfo = P.res("yT"), P.res("zT"), P.res("vfo")
    RXL = P.res("Xload")
    env = dict(locals())
    mixer = make_mixer(env) if "mix" in stages else None
    def do_tile(ti, t0, n):
        src = xT[:, :, t0:t0 + n].rearrange("j p t -> p j t")
        P.dma("sp", lambda e: e.dma_start(out=X[:, :, 0:n], in_=src), RXL, also=RX)
        if "ffn1" in stages:
            rmsnorm_stats(n)
            norm_apply("g1", n, lambda j: hT[:, j, 0:n], lambda j: RhT)
            ffn(wg1, wu1, wd1, n)
        if mixer is not None:
            mixer(ti, t0, n)
        if "ffn2" in stages:
            rmsnorm_stats(n)
            norm_apply("g2", n, lambda j: hT[:, j, 0:n], lambda j: RhT)
            ffn(wg2, wu2, wd2, n)
        dsty = yT[:, :, t0:t0 + n].rearrange("j p t -> p j t")
        P.dma("sp", lambda e: e.dma_start(out=dsty, in_=X[:, :, 0:n]), Ry, reads=RX)
        rmsnorm_stats(n)
        norm_apply("gf", n, lambda j: X[:, j, 0:n], lambda j: RX[j])
        dstz = zT[:, :, t0:t0 + n].rearrange("j p t -> p j t")
        P.dma("sp", lambda e: e.dma_start(out=dstz, in_=X[:, :, 0:n]), Rz, reads=RX)

    for ti, (t0, n) in enumerate(tiles):
        do_tile(ti, t0, n)
    P.wait_all("sp", [Ry, Rz, Rvfo, Rdbg])
    P.emit()
    st.close()
    nc._prog_stats = (P.n_instr, A.peak)
    return nc


def make_mixer(env):
    g = dict(env)
    nc, P, A = g["nc"], g["P"], g["A"]
    X, RX, hT, RhT, rstd, Rrstd = g["X"], g["RX"], g["hT"], g["RhT"], g["rstd"], g["Rrstd"]
    pp, Rpp, cst, Rcst, ppc, ppw, csl = g["pp"], g["Rpp"], g["cst"], g["Rcst"], g["ppc"], g["ppw"], g["csl"]
    ones, bones, identb, Rid, epsT, Reps = g["ones"], g["bones"], g["identb"], g["Rid"], g["epsT"], g["Reps"]
    omka, Romka = g["omka"], g["Romka"]
    ps_t, Rb = g["ps_t"], g["Rb"]
    TILE = g["TILE"]
    win, wglu, wups, wupr, wout = g["win"], g["wglu"], g["wups"], g["wupr"], g["wout"]
    vfT, vfo, Rvfo = g["vfT"], g["vfo"], g["Rvfo"]
    dbg_out = g["dbg_out"]
    rmsnorm_stats, norm_apply = g["rmsnorm_stats"], g["norm_apply"]

    def PB(i):
        return ps_t[:, i, :]

    wa2b = A.alloc([1024], BF16)
    g2ab = A.alloc([1024], BF16)
    g2bb = A.alloc([1024], BF16)
    v2b = A.alloc([1024], BF16)
    Rlw = [P.res("loraw") for _ in range(4)]
    P.dma("pool", lambda e: e.dma_start(out=wa2b, in_=g["wa2"][:, :]), Rlw[0])
    P.dma("pool", lambda e: e.dma_start(out=g2ab, in_=g["g2a"][:, :]), Rlw[1])
    P.dma("pool", lambda e: e.dma_start(out=g2bb[0:32, :], in_=g["g2b"][:, :]), Rlw[2])
    P.dma("pool", lambda e: e.dma_start(out=v2b[0:32, :], in_=g["v2d"][:, :]), Rlw[3])
    bstb = [A.alloc([8, 128], BF16) for _ in range(2)]
    Rbst = [P.res("bst") for _ in range(2)]
    P.dma("pool", lambda e: e.dma_start(out=bstb[0], in_=g["bre_d"].rearrange("p (a b) -> p a b", a=8)), Rbst[0])
    P.dma("pool", lambda e: e.dma_start(out=bstb[1], in_=g["bim_d"].rearrange("p (a b) -> p a b", a=8)), Rbst[1])
    bst3 = [A.alloc([8, 128], BF16) for _ in range(2)]
    P.dma("pool", lambda e: e.dma_start(out=bst3[0], in_=g["bre3_d"].rearrange("p (a b) -> p a b", a=8)), Rbst[0])
    P.dma("pool", lambda e: e.dma_start(out=bst3[1], in_=g["bim3_d"].rearrange("p (a b) -> p a b", a=8)), Rbst[1])
    cqb = [A.alloc([32, 64], BF16) for _ in range(2)]
    Rcq = P.res("cq")

    ST32 = A.alloc([8, 64])
    STb = A.alloc([8, 64], BF16)
    RST = [P.res("ST") for _ in range(8)]
    RSTb = [P.res("STb") for _ in range(8)]
    P.op("dve", lambda e: e.memset(ST32, 0.0), [], RST)
    P.op("dve", lambda e: e.memset(STb, 0.0), [], RSTb)
    carry = A.alloc([32])
    Rcar = [P.res("carry") for _ in range(32)]
    P.op("dve", lambda e: e.memset(carry, 0.0), [], Rcar)
    s5s = [A.alloc([32]) for _ in range(2)]
    Rs5 = P.res("s5state")
    P.op("dve", lambda e: e.memset(s5s[0], 0.0), [], [Rs5])
    P.op("dve", lambda e: e.memset(s5s[1], 0.0), [Rs5], [Rs5])

    rho = A.alloc([32])
    rho0 = A.alloc([32, 64])
    tcs = A.alloc([32, 64])
    tsn = A.alloc([32, 64])
    Rtab = P.res("s5tab")
    m0 = A.mark()
    c32 = [A.alloc([32, 64]) for _ in range(2)]
    Rc32 = [P.res("c32") for _ in range(2)]
    P.dma("sp", lambda e: e.dma_start(out=c32[0], in_=g["cre_d"].rearrange("p (a b) -> p a b", a=32)), Rc32[0])
    P.dma("sp", lambda e: e.dma_start(out=c32[1], in_=g["cim_d"].rearrange("p (a b) -> p a b", a=32)), Rc32[1])
    sc = [A.alloc([32]) for _ in range(12)]
    Rsc = P.res("s5scratch")
    lre, lim, ldt = ppw("lre"), ppw("lim"), ppw("ldt")
    dtv, th, k_, r1, sn1, cs1, nr, den, qre, qim, t1, t2 = sc
    TWO_PI = 2.0 * math.pi

    def dv(fn, rd=(), wr=None):
        P.op("dve", fn, [Rpp, Rsc] + list(rd), [Rsc] if wr is None else wr)

    def ac(fn, rd=(), wr=None):
        P.op("act", fn, [Rpp, Rsc] + list(rd), [Rsc] if wr is None else wr)

    ac(lambda e: e.activation(out=dtv, in_=ldt, func=AF.Exp))
    dv(lambda e: e.tensor_tensor(out=t1, in0=lre, in1=dtv, op=ALU.mult))
    ac(lambda e: e.activation(out=rho, in_=t1, func=AF.Exp), wr=[Rsc, Rtab])
    dv(lambda e: e.tensor_tensor(out=th, in0=lim, in1=dtv, op=ALU.mult))

    def sin_reduced(out, src, shift):
        i32 = t2.bitcast(mybir.dt.int32)
        dv(lambda e: e.tensor_scalar(out=k_, in0=src, scalar1=shift, scalar2=1.0 / TWO_PI, op0=ALU.add, op1=ALU.mult))
        dv(lambda e: e.tensor_copy(out=i32, in_=k_))
        dv(lambda e: e.tensor_copy(out=k_, in_=i32))
        dv(lambda e: e.scalar_tensor_tensor(out=r1, in0=k_, scalar=-TWO_PI, in1=src, op0=ALU.mult, op1=ALU.add))
        dv(lambda e: e.tensor_scalar(out=r1, in0=r1, scalar1=shift, scalar2=None, op0=ALU.add))
        dv(lambda e: e.tensor_scalar(out=k_, in0=r1, scalar1=math.pi, scalar2=-TWO_PI, op0=ALU.is_gt, op1=ALU.mult))
        dv(lambda e: e.tensor_tensor(out=r1, in0=r1, in1=k_, op=ALU.add))
        dv(lambda e: e.tensor_scalar(out=k_, in0=r1, scalar1=-math.pi, scalar2=TWO_PI, op0=ALU.is_lt, op1=ALU.mult))
        dv(lambda e: e.tensor_tensor(out=r1, in0=r1, in1=k_, op=ALU.add))
        dv(lambda e: e.tensor_scalar(out=r1, in0=r1, scalar1=-math.pi, scalar2=math.pi, op0=ALU.max, op1=ALU.min))
        ac(lambda e: e.activation(out=out, in_=r1, func=AF.Sin))

    sin_reduced(sn1, th, 0.0)
    sin_reduced(cs1, th, math.pi / 2)
    dv(lambda e: e.tensor_tensor(out=nr, in0=rho, in1=cs1, op=ALU.mult), rd=[Rtab])
    dv(lambda e: e.tensor_scalar(out=nr, in0=nr, scalar1=-1.0, scalar2=None, op0=ALU.add))
    dv(lambda e: e.tensor_tensor(out=t1, in0=rho, in1=sn1, op=ALU.mult), rd=[Rtab])
    dv(lambda e: e.tensor_tensor(out=den, in0=lre, in1=lre, op=ALU.mult))
    dv(lambda e: e.tensor_tensor(out=t2, in0=lim, in1=lim, op=ALU.mult))
    dv(lambda e: e.tensor_tensor(out=den, in0=den, in1=t2, op=ALU.add))
    dv(lambda e: e.reciprocal(out=den, in_=den))
    dv(lambda e: e.tensor_tensor(out=qre, in0=nr, in1=lre, op=ALU.mult))
    dv(lambda e: e.tensor_tensor(out=t2, in0=t1, in1=lim, op=ALU.mult))
    dv(lambda e: e.tensor_tensor(out=qre, in0=qre, in1=t2, op=ALU.add))
    dv(lambda e: e.tensor_tensor(out=qre, in0=qre, in1=den, op=ALU.mult))
    dv(lambda e: e.tensor_tensor(out=qim, in0=t1, in1=lre, op=ALU.mult))
    dv(lambda e: e.tensor_tensor(out=t2, in0=nr, in1=lim, op=ALU.mult))
    dv(lambda e: e.tensor_tensor(out=qim, in0=qim, in1=t2, op=ALU.subtract))
    dv(lambda e: e.tensor_tensor(out=qim, in0=qim, in1=den, op=ALU.mult))
    ctmp = [A.alloc([32, 64]) for _ in range(2)]
    qreb = qre.unsqueeze(2).to_broadcast([128, 32, 64])
    qimb = qim.unsqueeze(2).to_broadcast([128, 32, 64])
    dv(lambda e: e.tensor_tensor(out=ctmp[0], in0=c32[0], in1=qreb, op=ALU.mult), rd=Rc32)
    dv(lambda e: e.tensor_tensor(out=ctmp[1], in0=c32[1], in1=qimb, op=ALU.mult), rd=Rc32)
    dv(lambda e: e.tensor_tensor(out=cqb[0], in0=ctmp[0], in1=ctmp[1], op=ALU.subtract), wr=[Rsc, Rcq])
    dv(lambda e: e.tensor_tensor(out=ctmp[0], in0=c32[0], in1=qimb, op=ALU.mult), rd=Rc32)
    dv(lambda e: e.tensor_tensor(out=ctmp[1], in0=c32[1], in1=qreb, op=ALU.mult), rd=Rc32)
    dv(lambda e: e.tensor_tensor(out=ctmp[0], in0=ctmp[0], in1=ctmp[1], op=ALU.add))
    dv(lambda e: e.tensor_scalar(out=cqb[1], in0=ctmp[0], scalar1=-1.0, scalar2=None, op0=ALU.mult), wr=[Rsc, Rcq])
    dv(lambda e: e.tensor_copy(out=tcs[:, :, 0:1], in_=cs1.unsqueeze(2)), wr=[Rsc, Rtab])
    dv(lambda e: e.tensor_copy(out=tsn[:, :, 0:1], in_=sn1.unsqueeze(2)), wr=[Rsc, Rtab])
    tt = [A.alloc([32, 32]) for _ in range(2)]
    mlen = 1
    while mlen < 64:
        cm = tcs[:, :, mlen - 1:mlen].to_broadcast([128, 32, mlen])
        sm = tsn[:, :, mlen - 1:mlen].to_broadcast([128, 32, mlen])
        lo_c, lo_s = tcs[:, :, 0:mlen], tsn[:, :, 0:mlen]
        hi_c, hi_s = tcs[:, :, mlen:2 * mlen], tsn[:, :, mlen:2 * mlen]
        a0, a1 = tt[0][:, :, 0:mlen], tt[1][:, :, 0:mlen]
        dv(lambda e, lo_c=lo_c, cm=cm, a0=a0: e.tensor_tensor(out=a0, in0=lo_c, in1=cm, op=ALU.mult), rd=[Rtab])
        dv(lambda e, lo_s=lo_s, sm=sm, a1=a1: e.tensor_tensor(out=a1, in0=lo_s, in1=sm, op=ALU.mult), rd=[Rtab])
        dv(lambda e, hi_c=hi_c, a0=a0, a1=a1: e.tensor_tensor(out=hi_c, in0=a0, in1=a1, op=ALU.subtract), rd=[Rtab], wr=[Rsc, Rtab])
        dv(lambda e, lo_c=lo_c, sm=sm, a0=a0: e.tensor_tensor(out=a0, in0=lo_c, in1=sm, op=ALU.mult), rd=[Rtab])
        dv(lambda e, lo_s=lo_s, cm=cm, a1=a1: e.tensor_tensor(out=a1, in0=lo_s, in1=cm, op=ALU.mult), rd=[Rtab])
        dv(lambda e, hi_s=hi_s, a0=a0, a1=a1: e.tensor_tensor(out=hi_s, in0=a0, in1=a1, op=ALU.add), rd=[Rtab], wr=[Rsc, Rtab])
        mlen *= 2
    for hf in range(4):
        gsl = slice(8 * hf, 8 * hf + 8)
        dv(lambda e, gsl=gsl: e.tensor_copy(out=rho0[:, gsl, :].rearrange("p (q j) t -> p q j t", q=4),
                                            in_=rho[:, gsl].rearrange("p (j q) -> p q j", q=4).unsqueeze(3).to_broadcast([128, 4, 2, 64])),
           rd=[Rtab], wr=[Rsc, Rtab])
    dv(lambda e: e.memset(rho0[:, :, 0:1], 0.0), rd=[Rtab], wr=[Rsc, Rtab])
    P.barrier()
    A.release(m0)

    wbuf = [None] * 3
    Rwb = [None] * 3
    wbi = [0]
    winv = win.rearrange("(j p) c -> p j c", p=128)

    def proj(col0, ncols, n, bk, rb):
        b = wbi[0] % 3
        wbi[0] += 1
        wb_ = wbuf[b]
        P.dma("pool", lambda e: e.dma_start(out=wb_[:, :, 0:ncols], in_=winv[:, :, col0:col0 + ncols]), Rwb[b])
        for j in range(NDC):
            P.op("pe", lambda e, j=j: e.matmul(bk[0:ncols, 0:n], lhsT=wb_[:, j, 0:ncols], rhs=hT[:, j, 0:n],
                                               start=(j == 0), stop=(j == NDC - 1)), [Rwb[b], RhT], [rb])

    stage = [None] * 2
    Rstg = [None] * 2
    dtmp = [None] * 2
    Rdt = [None] * 2
    sti = [0]

    def shifted(col0, ncols, n, mu_ap, ci, out_ap, out_res):
        s = sti[0] % 2
        sti[0] += 1
        bi = (wbi[0]) % 2
        bk, rb = PB(bi), Rb[bi]
        proj(col0, ncols, n, bk, rb)
        sg, rs = stage[s], Rstg[s]
        dt_, rdt_ = dtmp[s], Rdt[s]
        P.op("act", lambda e: e.activation(out=sg[0:ncols, 1:n + 1], in_=bk[0:ncols, 0:n], func=AF.Copy), [rb], [rs])
        P.op("dve", lambda e: e.tensor_copy(out=sg[0:ncols, 0:1], in_=carry[0:ncols, ci:ci + 1]), [Rcar[ci], rs], [rs])
        P.op("dve", lambda e: e.tensor_tensor(out=dt_[0:ncols, 0:n], in0=sg[0:ncols, 0:n], in1=sg[0:ncols, 1:n + 1],
                                              op=ALU.subtract), [rs], [rdt_])
        P.op("dve", lambda e: e.scalar_tensor_tensor(out=out_ap, in0=dt_[0:ncols, 0:n], scalar=mu_ap,
                                                     in1=sg[0:ncols, 1:n + 1], op0=ALU.mult, op1=ALU.add),
             [rdt_, rs, Rpp], [out_res])
        P.op("dve", lambda e: e.tensor_copy(out=carry[0:ncols, ci:ci + 1], in_=sg[0:ncols, n:n + 1]), [rs], [Rcar[ci]])

    def mixer(ti, t0, n):
        nch = n // 64
        mk = A.mark()
        for b_ in range(3):
            wbuf[b_] = A.alloc([NDC, 128], BF16)
            Rwb[b_] = P.pres("wbuf%d" % b_)
        for b_ in range(2):
            stage[b_] = A.alloc([TILE + 1])
            Rstg[b_] = P.res("stage")
            dtmp[b_] = A.alloc([TILE])
            Rdt[b_] = P.res("dtmp")
        rmsnorm_stats(n)
        norm_apply("gm", n, lambda j: hT[:, j, 0:n], lambda j: RhT)
        ybT = A.alloc([8, TILE], BF16)
        yaT = A.alloc([8, TILE], BF16)
        Ryb = [P.res("yb") for _ in range(8)]
        Rya = [P.res("ya") for _ in range(8)]

        mr = A.mark()
        xwa = A.alloc([TILE])
        wab = A.alloc([TILE], BF16)
        sg1b = A.alloc([TILE], BF16)
        sg2b = A.alloc([TILE], BF16)
        xvrb = A.alloc([TILE], BF16)
        tmpg = A.alloc([TILE])
        Rl = [P.res("lora") for _ in range(6)]
        shifted(COL_WA, 128, n, ppc("muwa"), 24, xwa[:, 0:n], Rl[0])
        P.op("act", lambda e: e.activation(out=wab[0:64, 0:n], in_=xwa[0:64, 0:n], func=AF.Tanh), [Rl[0]], [Rl[1]])
        P.op("act", lambda e: e.activation(out=wab[64:128, 0:n], in_=xwa[64:128, 0:n], func=AF.Copy), [Rl[0]], [Rl[1]])
        shifted(COL_G, 128, n, ppc("mug1"), 25, tmpg[:, 0:n], Rl[5])
        P.op("act", lambda e: e.activation(out=sg1b[:, 0:n], in_=tmpg[:, 0:n], func=AF.Sigmoid), [Rl[5]], [Rl[2]])
        shifted(COL_G + 128, 32, n, ppc("mug2", parts=32), 26, tmpg[0:32, 0:n], Rl[5])
        P.op("act", lambda e: e.activation(out=sg2b[0:32, 0:n], in_=tmpg[0:32, 0:n], func=AF.Sigmoid), [Rl[5]], [Rl[3]])
        shifted(COL_VR, 32, n, ppc("muvr", parts=32), 27, tmpg[0:32, 0:n], Rl[5])
        P.op("act", lambda e: e.activation(out=xvrb[0:32, 0:n], in_=tmpg[0:32, 0:n], func=AF.Copy), [Rl[5]], [Rl[4]])

        if MIX_STOP == 1:
            P.barrier(); A.release(mk); return
        f32n = ["xr", "xk", "xv", "sg", "iclr", "gate", "sv", "vf", "v32", "kkn", "tt", "kmod", "bonus", "cl", "pm",
                "pinv", "pprev", "t1", "t2", "OT"]
        alias = {"OT": "xv", "cl": "xv", "bonus": "xk", "tt": "pprev", "sv": "pinv", "vf": "pm"}
        W = {nm: A.alloc([TILE]) for nm in f32n if nm not in alias and nm not in ("t1", "t2")}
        R = {nm: (P.pres("pm_vf") if nm == "pm" else P.res(nm)) for nm in f32n if nm not in alias and nm not in ("t1", "t2")}
        W["t1"], R["t1"] = xwa, Rl[0]
        W["t2"], R["t2"] = tmpg, Rl[5]
        for k_a, v_a in alias.items():
            W[k_a] = W[v_a]
            R[k_a] = R[v_a]
        arb = A.alloc([8, 2, 64], BF16)
        btb = A.alloc([TILE], BF16)
        ktb = A.alloc([TILE], BF16)
        bhb = A.alloc([TILE], BF16)
        khb = A.alloc([TILE], BF16)
        vbb = A.alloc([TILE], BF16)
        Rar, Rbt, Rkt, Rbh, Rkh, Rvb = [P.res(x) for x in ("ar", "bt", "kt", "bh", "kh", "vb")]
        tok = A.alloc([4, 3, 128], BF16, parts=64)
        A1 = A.alloc([2, 4, 256], BF16, parts=64)
        Qb = [A.alloc([2, 4, 64], BF16, parts=64) for _ in range(2)]
        QTb = [A.alloc([2, 4, 64], BF16, parts=64) for _ in range(2)]
        Tm32 = A.alloc([2, 4, 64], parts=64)
        Tmb = A.alloc([2, 4, 64], BF16, parts=64)
        Xb = A.alloc([128], BF16, parts=64)
        Ub = A.alloc([128], BF16, parts=64)
        Xs = A.alloc([128], parts=64)
        RXs = P.res("Xs")
        Rtok, RA1, RTm32, RTmb, RXb, RUb = [P.res(x) for x in ("tok", "A1", "Tm32", "Tmb", "Xb", "Ub")]
        RQ = [P.res("Q") for _ in range(2)]
        RQT = [P.res("QT") for _ in range(2)]
        m1c = csl("m1", 64)
        mslc = csl("msl", 64)
        ifc = csl("ifree", 64)
        rmask = csl("rmask")

        def dvo(fn, rd, wr):
            P.op("dve", fn, rd, wr)

        def aco(fn, rd, wr):
            P.op("act", fn, rd, wr)

        for c in range(8):
            cols = slice(128 * c, 128 * c + 128)
            shifted(COL_R + 128 * c, 128, n, ppc("mur", c), c, W["xr"][:, 0:n], R["xr"])
            shifted(COL_K + 128 * c, 128, n, ppc("muk", c), 8 + c, W["xk"][:, 0:n], R["xk"])
            shifted(COL_V + 128 * c, 128, n, ppc("muv", c), 16 + c, W["xv"][:, 0:n], R["xv"])
            P.dma("sp", lambda e, c=c: e.dma_start(out=vfo[c, :, t0:t0 + n], in_=W["xv"][:, 0:n]), Rvfo, reads=[R["xv"]])
            P.dma("sp", lambda e, c=c: e.dma_start(out=W["vf"][:, 0:n], in_=vfT[c, :, t0:t0 + n]), R["vf"])
            P.op("pe", lambda e, cols=cols: e.matmul(PB(2)[:, 0:n], lhsT=wa2b[0:64, cols], rhs=wab[0:64, 0:n], start=True,
                                                     stop=True), [Rlw[0], Rl[1]], [Rb[2]])
            aco(lambda e, c=c: e.activation(out=W["sg"][:, 0:n], in_=PB(2)[:, 0:n], func=AF.Sigmoid, bias=ppc("w0", c)),
                [Rb[2], Rpp], [R["sg"]])
            P.op("pe", lambda e, cols=cols: e.matmul(PB(3)[:, 0:n], lhsT=wa2b[64:128, cols], rhs=wab[64:128, 0:n],
                                                     start=True, stop=True), [Rlw[0], Rl[1]], [Rb[3]])
            aco(lambda e, c=c: e.activation(out=W["iclr"][:, 0:n], in_=PB(3)[:, 0:n], func=AF.Sigmoid, bias=ppc("a0", c)),
                [Rb[3], Rpp], [R["iclr"]])
            P.op("pe", lambda e, cols=cols: e.matmul(PB(2)[:, 0:n], lhsT=g2ab[:, cols], rhs=sg1b[:, 0:n], start=True,
                                                     stop=False), [Rlw[1], Rl[2]], [Rb[2]])
            P.op("pe", lambda e, cols=cols: e.matmul(PB(2)[:, 0:n], lhsT=g2bb[0:32, cols], rhs=sg2b[0:32, 0:n], start=False,
                                                     stop=True), [Rlw[2], Rl[3]], [Rb[2]])
            aco(lambda e: e.activation(out=W["gate"][:, 0:n], in_=PB(2)[:, 0:n], func=AF.Copy), [Rb[2]], [R["gate"]])
            P.op("pe", lambda e, cols=cols: e.matmul(PB(3)[:, 0:n], lhsT=v2b[0:32, cols], rhs=xvrb[0:32, 0:n], start=True,
                                                     stop=True), [Rlw[3], Rl[4]], [Rb[3]])
            aco(lambda e, c=c: e.activation(out=W["sv"][:, 0:n], in_=PB(3)[:, 0:n], func=AF.Sigmoid, bias=ppc("v0", c)),
                [Rb[3], Rpp], [R["sv"]])
            dvo(lambda e: e.tensor_tensor(out=W["t1"][:, 0:n], in0=W["vf"][:, 0:n], in1=W["xv"][:, 0:n], op=ALU.subtract),
                [R["vf"], R["xv"]], [R["t1"]])
            dvo(lambda e: e.tensor_tensor(out=W["t1"][:, 0:n], in0=W["t1"][:, 0:n], in1=W["sv"][:, 0:n], op=ALU.mult),
                [R["t1"], R["sv"]], [R["t1"]])
            dvo(lambda e: e.scalar_tensor_tensor(out=W["v32"][:, 0:n], in0=W["t1"][:, 0:n], scalar=ppc("flag"),
                                                 in1=W["xv"][:, 0:n], op0=ALU.mult, op1=ALU.add),
                [R["t1"], R["xv"], Rpp], [R["v32"]])
            aco(lambda e: e.activation(out=vbb[:, 0:n], in_=W["v32"][:, 0:n], func=AF.Copy), [R["v32"]], [Rvb])
            dvo(lambda e, c=c: e.tensor_scalar(out=W["t1"][:, 0:n], in0=W["xk"][:, 0:n], scalar1=ppc("kk", c), scalar2=None,
                                               op0=ALU.mult), [R["xk"], Rpp, R["t1"]], [R["t1"]])
            aco(lambda e: e.activation(out=W["t2"][:, 0:n], in_=W["t1"][:, 0:n], func=AF.Square), [R["t1"]], [R["t2"]])
            P.op("pe", lambda e: e.matmul(PB(2)[:, 0:n], lhsT=bones, rhs=W["t2"][:, 0:n], start=True, stop=True),
                 [Rcst, R["t2"]], [Rb[2]])
            aco(lambda e: e.activation(out=W["t2"][:, 0:n], in_=PB(2)[:, 0:n], func=AF.Sqrt, bias=epsT[:, 1:2]),
                [Rb[2], Reps], [R["t2"]])
            dvo(lambda e: e.reciprocal(out=W["t2"][:, 0:n], in_=W["t2"][:, 0:n]), [R["t2"]], [R["t2"]])
            dvo(lambda e: e.tensor_tensor(out=W["kkn"][:, 0:n], in0=W["t1"][:, 0:n], in1=W["t2"][:, 0:n], op=ALU.mult),
                [R["t1"], R["t2"]], [R["kkn"]])
            dvo(lambda e, c=c: e.tensor_scalar(out=W["tt"][:, 0:n], in0=W["iclr"][:, 0:n], scalar1=ppc("ka", c),
                                               scalar2=omka[:, c:c + 1], op0=ALU.mult, op1=ALU.add),
                [R["iclr"], Rpp, Romka], [R["tt"]])
            dvo(lambda e: e.tensor_tensor(out=W["kmod"][:, 0:n], in0=W["xk"][:, 0:n], in1=W["tt"][:, 0:n], op=ALU.mult),
                [R["xk"], R["tt"]], [R["kmod"]])
            dvo(lambda e, c=c: e.scalar_tensor_tensor(out=W["t1"][:, 0:n], in0=W["xr"][:, 0:n], scalar=ppc("rk", c),
                                                      in1=W["kmod"][:, 0:n], op0=ALU.mult, op1=ALU.mult),
                [R["xr"], R["kmod"], Rpp, R["t1"]], [R["t1"]])
            P.op("pe", lambda e: e.matmul(PB(3)[:, 0:n], lhsT=bones, rhs=W["t1"][:, 0:n], start=True, stop=True),
                 [Rcst, R["t1"]], [Rb[3]])
            dvo(lambda e: e.tensor_tensor(out=W["bonus"][:, 0:n], in0=PB(3)[:, 0:n], in1=W["v32"][:, 0:n], op=ALU.mult),
                [Rb[3], R["v32"]], [R["bonus"]])
            dvo(lambda e: e.tensor_tensor_scan(out=W["cl"][:, 0:n], data0=rmask[:, 0:n], data1=W["sg"][:, 0:n], initial=0.0,
                                               op0=ALU.mult, op1=ALU.add), [Rcst, R["sg"]], [R["cl"]])
            aco(lambda e: e.activation(out=W["pm"][:, 0:n], in_=W["cl"][:, 0:n], func=AF.Exp, scale=-C0), [R["cl"]], [R["pm"]])
            aco(lambda e: e.activation(out=W["pinv"][:, 0:n], in_=W["cl"][:, 0:n], func=AF.Exp, scale=C0), [R["cl"]], [R["pinv"]])
            dvo(lambda e: e.tensor_tensor(out=W["t2"][:, 0:n], in0=W["cl"][:, 0:n], in1=W["sg"][:, 0:n], op=ALU.subtract),
                [R["cl"], R["sg"], R["t2"]], [R["t2"]])
            aco(lambda e: e.activation(out=W["pprev"][:, 0:n], in_=W["t2"][:, 0:n], func=AF.Exp, scale=-C0),
                [R["t2"]], [R["pprev"]])
            v3 = lambda ap: ap[:, 0:n].rearrange("p (c l) -> p c l", l=64)
            dvo(lambda e: e.scalar_tensor_tensor(out=arb[:, 0:nch, 0, :], in0=v3(W["kkn"]), scalar=-1.0, in1=v3(W["pprev"]),
                                                 op0=ALU.mult, op1=ALU.mult), [R["kkn"], R["pprev"]], [Rar])
            dvo(lambda e: e.tensor_tensor(out=arb[:, 0:nch, 1, :], in0=v3(W["xr"]), in1=v3(W["pm"]), op=ALU.mult),
                [R["xr"], R["pm"], Rar], [Rar])
            dvo(lambda e: e.tensor_tensor(out=W["t1"][:, 0:n], in0=W["kkn"][:, 0:n], in1=W["iclr"][:, 0:n], op=ALU.mult),
                [R["kkn"], R["iclr"], R["t1"]], [R["t1"]])
            dvo(lambda e: e.tensor_tensor(out=btb[:, 0:n], in0=W["t1"][:, 0:n], in1=W["pinv"][:, 0:n], op=ALU.mult),
                [R["t1"], R["pinv"]], [Rbt])
            dvo(lambda e: e.tensor_tensor(out=ktb[:, 0:n], in0=W["kmod"][:, 0:n], in1=W["pinv"][:, 0:n], op=ALU.mult),
                [R["kmod"], R["pinv"]], [Rkt])
            plb = v3(W["pm"])[:, :, 63:64].to_broadcast([128, nch, 64])
            dvo(lambda e: e.tensor_tensor(out=v3(bhb), in0=v3(btb), in1=plb, op=ALU.mult), [Rbt, R["pm"]], [Rbh])
            dvo(lambda e: e.tensor_tensor(out=v3(khb), in0=v3(ktb), in1=plb, op=ALU.mult), [Rkt, R["pm"]], [Rkh])

            if MIX_STOP == 2:
                continue
            for q0 in range(0, nch, 4):
                nq = min(4, nch - q0)
                for i in range(nq):
                    cc = slice((q0 + i) * 64, (q0 + i) * 64 + 64)
                    pbf = PB(6 + i // 2).bitcast(BF16)[0:64, 0:768].rearrange("p (a b c) -> p a b c", a=2, b=3)
                    for k3, (src, rs) in enumerate(((bhb, Rbh), (khb, Rkh), (vbb, Rvb))):
                        P.op("pe", lambda e, src=src, cc=cc, pbf=pbf, i=i, k3=k3: e.transpose(
                            pbf[:, i % 2, k3, :], src[:, cc], identb), [rs, Rid], [Rb[6 + i // 2]])
                for hb in range((nq + 1) // 2):
                    nn = min(2, nq - 2 * hb)
                    pbf = PB(6 + hb).bitcast(BF16)[0:64, 0:768].rearrange("p (a b c) -> p a b c", a=2, b=3)
                    aco(lambda e, hb=hb, nn=nn, pbf=pbf: e.activation(out=tok[:, 2 * hb:2 * hb + nn], in_=pbf[:, 0:nn],
                                                                       func=AF.Copy), [Rb[6 + hb]], [Rtok])
                for i in range(nq):
                    ci = q0 + i
                    cc = slice(ci * 64, ci * 64 + 64)
                    for h2 in range(2):
                        hs = slice(64 * h2, 64 * h2 + 64)
                        bka = 2 * h2 + i // 2
                        pa = PB(bka)[0:64, (i % 2) * 256:(i % 2) * 256 + 256].rearrange("p (x y) -> p x y", x=2)
                        pn = PB(4 + h2)[0:64, i * 64:i * 64 + 64]
                        rhs_ar = arb[hs, ci, :, :].rearrange("p a b -> p (a b)")
                        P.op("pe", lambda e, hs=hs, cc=cc, pa=pa, rhs_ar=rhs_ar: e.matmul(
                            pa[:, 0, :], lhsT=btb[hs, cc], rhs=rhs_ar, start=True, stop=True), [Rbt, Rar], [Rb[bka]])
                        P.op("pe", lambda e, hs=hs, cc=cc, pa=pa, rhs_ar=rhs_ar: e.matmul(
                            pa[:, 1, :], lhsT=ktb[hs, cc], rhs=rhs_ar, start=True, stop=True), [Rkt, Rar], [Rb[bka]])
                        P.op("pe", lambda e, hs=hs, cc=cc, pn=pn, ci=ci: e.matmul(
                            pn, lhsT=arb[hs, ci, 0, :], rhs=btb[hs, cc], start=True, stop=True), [Rbt, Rar], [Rb[4 + h2]])
                for h2 in range(2):
                    dvo(lambda e, h2=h2, nq=nq: e.tensor_tensor(
                        out=A1[:, h2, 0:nq, :],
                        in0=ps_t[0:64, 2 * h2:2 * h2 + 2, :].rearrange("p b (a x) -> p (b a) x", a=2)[:, 0:nq, :],
                        in1=m1c[:, 0:256].unsqueeze(1).to_broadcast([64, nq, 256]), op=ALU.mult),
                        [Rb[2 * h2], Rb[2 * h2 + 1], Rcst, RA1], [RA1])
                dvo(lambda e, nq=nq: e.tensor_tensor(
                    out=Qb[0][:, :, 0:nq, :], in0=ps_t[0:64, 4:6, 0:256].rearrange("p h (i t) -> p h i t", i=4)[:, :, 0:nq, :],
                    in1=mslc[:, 0:64].unsqueeze(1).unsqueeze(1).to_broadcast([64, 2, nq, 64]), op=ALU.mult),
                    [Rb[4], Rb[5], Rcst], [RQ[0]])
                A1v = A1.rearrange("p h i (x y t) -> p h i x y t", x=2, y=2)
                ntv = A1v[:, :, :, 0, 0, :]
                arbT = A1v[:, :, :, 0, 1, :]
                aktT = A1v[:, :, :, 1, 0, :]
                arkT = A1v[:, :, :, 1, 1, :]
                ifb = ifc[:, 0:64].unsqueeze(1).unsqueeze(1).to_broadcast([64, 2, nq, 64])
                sq_ = lambda ap, nq=nq: ap[:, :, 0:nq, :]
                dvo(lambda e, nq=nq: e.tensor_copy(out=sq_(QTb[0]), in_=ntv[:, :, 0:nq, :]), [RA1], [RQT[0]])
                dvo(lambda e, ifb=ifb: e.tensor_tensor(out=sq_(Tm32), in0=sq_(QTb[0]), in1=ifb, op=ALU.add), [RQT[0], Rcst], [RTm32])
                aco(lambda e: e.activation(out=sq_(Tmb), in_=sq_(Tm32), func=AF.Copy), [RTm32], [RTmb])
                cur = 0
                bv = lambda k: PB(k)[0:64, :].rearrange("p (h i t) -> p h i t", h=2, i=4)
                for lvl in range(5):
                    nxt = 1 - cur
                    last = (lvl == 4)
                    pq, pqt, pt = bv(0), bv(1), bv(2)
                    Qc, QTc, Qn, QTn = Qb[cur], QTb[cur], Qb[nxt], QTb[nxt]
                    for i in range(nq):
                        for h2 in range(2):
                            P.op("pe", lambda e, i=i, h2=h2, pq=pq, Qc=Qc, QTc=QTc: e.matmul(
                                pq[:, h2, i, :], lhsT=QTc[:, h2, i, :], rhs=Qc[:, h2, i, :], start=True, stop=True),
                                [RQ[cur], RQT[cur]], [Rb[0]])
                    if not last:
                        for i in range(nq):
                            for h2 in range(2):
                                P.op("pe", lambda e, i=i, h2=h2, pqt=pqt, Qc=Qc, QTc=QTc: e.matmul(
                                    pqt[:, h2, i, :], lhsT=Qc[:, h2, i, :], rhs=QTc[:, h2, i, :], start=True, stop=True),
                                    [RQ[cur], RQT[cur]], [Rb[1]])
                    aco(lambda e, Qn=Qn, pq=pq: e.activation(out=sq_(Qn), in_=sq_(pq), func=AF.Copy), [Rb[0]], [RQ[nxt]])
                    if not last:
                        dvo(lambda e, QTn=QTn, pqt=pqt: e.tensor_copy(out=sq_(QTn), in_=sq_(pqt)), [Rb[1]], [RQT[nxt]])
                    for i in range(nq):
                        for h2 in range(2):
                            P.op("pe", lambda e, i=i, h2=h2, pt=pt, Qn=Qn: e.matmul(
                                pt[:, h2, i, :], lhsT=Qn[:, h2, i, :], rhs=Tmb[:, h2, i, :], start=True, stop=True),
                                [RQ[nxt], RTmb], [Rb[2]])
                    dvo(lambda e, pt=pt: e.tensor_tensor(out=sq_(Tm32), in0=sq_(pt), in1=sq_(Tm32), op=ALU.add), [Rb[2], RTm32], [RTm32])
                    aco(lambda e: e.activation(out=sq_(Tmb), in_=sq_(Tm32), func=AF.Copy), [RTm32], [RTmb])
                    cur = nxt
                Xv = Xb.rearrange("p (h v) -> p h v", h=2)
                Uv = Ub.rearrange("p (h v) -> p h v", h=2)
                Xsv = Xs.rearrange("p (h v) -> p h v", h=2)
                px2 = PB(5)[0:64, 128:256].rearrange("p (h v) -> p h v", h=2)
                pu = PB(5)[0:64, 256:384].rearrange("p (h v) -> p h v", h=2)
                pst = PB(6)[:, 0:64]
                pxb = ps_t[0:64, 5:8:2, 0:64]
                for i in range(nq):
                    ci = q0 + i
                    for h2 in range(2):
                        hs = slice(64 * h2, 64 * h2 + 64)
                        bx = 5 + 2 * h2
                        P.op("pe", lambda e, hs=hs, h2=h2, ci=ci, bx=bx, c=c: e.matmul(
                            PB(bx)[0:64, 0:64], lhsT=arb[hs, ci, 0, :], rhs=STb[hs, c, :], start=True, stop=True),
                            [Rar, RSTb[c]], [Rb[bx]])
                        P.op("pe", lambda e, hs=hs, h2=h2, i=i: e.matmul(
                            px2[:, h2, :], lhsT=aktT[:, h2, i, :], rhs=tok[:, i, 2, hs], start=True, stop=True),
                            [RA1, Rtok], [Rb[5]])
                    aco(lambda e: e.activation(out=Xsv, in_=pxb, func=AF.Copy), [Rb[5], Rb[7]], [RXs])
                    dvo(lambda e: e.tensor_tensor(out=Xv, in0=px2, in1=Xsv, op=ALU.add), [Rb[5], RXs], [RXb])
                    for h2 in range(2):
                        P.op("pe", lambda e, h2=h2, i=i: e.matmul(
                            pu[:, h2, :], lhsT=Tmb[:, h2, i, :], rhs=Xv[:, h2, :], start=True, stop=True),
                            [RTmb, RXb], [Rb[5]])
                    aco(lambda e: e.activation(out=Uv, in_=pu, func=AF.Copy), [Rb[5]], [RUb])
                    for h2 in range(2):
                        hs = slice(64 * h2, 64 * h2 + 64)
                        bo = 4 if h2 == 0 else 7
                        po1 = PB(bo)[hs, 64 + i * 64:128 + i * 64]
                        po2 = PB(4)[hs, 320 + i * 32:320 + i * 32 + 32] if False else PB(3)[hs, i * 64:i * 64 + 64]
                        P.op("pe", lambda e, hs=hs, ci=ci, po1=po1, c=c: e.matmul(
                            po1, lhsT=STb[hs, c, :], rhs=arb[hs, ci, 1, :], start=True, stop=True),
                            [RSTb[c], Rar], [Rb[bo]])
                        P.op("pe", lambda e, hs=hs, h2=h2, i=i, po2=po2: e.matmul(
                            po2, lhsT=Uv[:, h2, :], rhs=arbT[:, h2, i, :], start=True, stop=False),
                            [RUb, RA1], [Rb[3]])
                        P.op("pe", lambda e, hs=hs, h2=h2, i=i, po2=po2: e.matmul(
                            po2, lhsT=tok[:, i, 2, hs], rhs=arkT[:, h2, i, :], start=False, stop=True),
                            [Rtok, RA1], [Rb[3]])
                    for h2 in range(2):
                        hs = slice(64 * h2, 64 * h2 + 64)
                        P.op("pe", lambda e, hs=hs, h2=h2, i=i: e.matmul(
                            pst[hs, :], lhsT=tok[:, i, 0, hs], rhs=Uv[:, h2, :], start=True, stop=False),
                            [Rtok, RUb], [Rb[6]])
                        P.op("pe", lambda e, hs=hs, h2=h2, i=i: e.matmul(
                            pst[hs, :], lhsT=tok[:, i, 1, hs], rhs=tok[:, i, 2, hs], start=False, stop=True),
                            [Rtok], [Rb[6]])
                    plc = W["pm"][:, ci * 64 + 63:ci * 64 + 64]
                    dvo(lambda e, plc=plc, c=c: e.scalar_tensor_tensor(out=ST32[:, c, :], in0=ST32[:, c, :], scalar=plc,
                                                                       in1=pst, op0=ALU.mult, op1=ALU.add),
                        [RST[c], R["pm"], Rb[6]], [RST[c]])
                    aco(lambda e, c=c: e.activation(out=STb[:, c, :], in_=ST32[:, c, :], func=AF.Copy), [RST[c]], [RSTb[c]])
                qs = slice(q0 * 64, (q0 + nq) * 64)
                aco(lambda e, qs=qs, nq=nq: e.activation(out=W["OT"][0:64, qs], in_=PB(4)[0:64, 64:64 + nq * 64], func=AF.Copy),
                    [Rb[4]], [R["OT"]])
                aco(lambda e, qs=qs, nq=nq: e.activation(out=W["OT"][64:128, qs], in_=PB(7)[64:128, 64:64 + nq * 64], func=AF.Copy),
                    [Rb[7], R["OT"]], [R["OT"]])
                dvo(lambda e, qs=qs, nq=nq: e.tensor_tensor(out=W["OT"][:, qs], in0=PB(3)[:, 0:nq * 64], in1=W["OT"][:, qs], op=ALU.add),
                    [Rb[3], R["OT"]], [R["OT"]])
            P.op("pe", lambda e: e.matmul(PB(2)[:, 0:n], lhsT=bones, rhs=W["OT"][:, 0:n], start=True, stop=True),
                 [Rcst, R["OT"]], [Rb[2]])
            dvo(lambda e: e.scalar_tensor_tensor(out=W["t1"][:, 0:n], in0=PB(2)[:, 0:n], scalar=-1.0 / 64, in1=W["OT"][:, 0:n],
                                                 op0=ALU.mult, op1=ALU.add), [Rb[2], R["OT"], R["t1"]], [R["t1"]])
            aco(lambda e: e.activation(out=W["t2"][:, 0:n], in_=W["t1"][:, 0:n], func=AF.Square), [R["t1"], R["t2"]], [R["t2"]])
            P.op("pe", lambda e: e.matmul(PB(3)[:, 0:n], lhsT=bones, rhs=W["t2"][:, 0:n], start=True, stop=True),
                 [Rcst, R["t2"]], [Rb[3]])
            aco(lambda e: e.activation(out=W["t2"][:, 0:n], in_=PB(3)[:, 0:n], func=AF.Sqrt, bias=epsT[:, 2:3],
                                       scale=1.0 / 64), [Rb[3], Reps, R["t2"]], [R["t2"]])
            dvo(lambda e: e.reciprocal(out=W["t2"][:, 0:n], in_=W["t2"][:, 0:n]), [R["t2"]], [R["t2"]])
            dvo(lambda e: e.tensor_tensor(out=W["t1"][:, 0:n], in0=W["t1"][:, 0:n], in1=W["t2"][:, 0:n], op=ALU.mult),
                [R["t1"], R["t2"]], [R["t1"]])
            dvo(lambda e, c=c: e.tensor_scalar(out=W["t1"][:, 0:n], in0=W["t1"][:, 0:n], scalar1=ppc("lw", c),
                                               scalar2=ppc("lb", c), op0=ALU.mult, op1=ALU.add), [R["t1"], Rpp], [R["t1"]])
            dvo(lambda e: e.tensor_tensor(out=W["t1"][:, 0:n], in0=W["t1"][:, 0:n], in1=W["bonus"][:, 0:n], op=ALU.add),
                [R["t1"], R["bonus"]], [R["t1"]])
            dvo(lambda e, c=c: e.tensor_tensor(out=ybT[:, c, 0:n], in0=W["t1"][:, 0:n], in1=W["gate"][:, 0:n], op=ALU.mult),
                [R["t1"], R["gate"]], [Ryb[c]])
            if DEBUG_TENSOR:
                dvo(lambda e, c=c: e.tensor_copy(out=ybT[:, c, 0:n], in_=W[DEBUG_TENSOR][:, 0:n]), [R[DEBUG_TENSOR]], [Ryb[c]])
        P.barrier()
        A.release(mr)

        if MIX_STOP in (2, 3):
            P.barrier(); A.release(mk); return
        ms = A.mark()
        u32 = A.alloc([8, TILE])
        ub = A.alloc([8, TILE], BF16)
        Ru = [P.res("u") for _ in range(8)]
        for c in range(8):
            bi = c % 2
            proj(COL_U + 128 * c, 128, n, PB(bi), Rb[bi])
            aco(lambda e, c=c, bi=bi: e.activation(out=u32[:, c, 0:n], in_=PB(bi)[:, 0:n], func=AF.Copy), [Rb[bi]], [Ru[c]])
            dvo(lambda e, c=c: e.tensor_copy(out=ub[:, c, 0:n], in_=u32[:, c, 0:n]), [Ru[c]], [Ru[c]])
        ya32 = A.alloc([8, TILE])
        Rya32 = P.res("ya32")
        HG = 8
        dpr = A.alloc([HG, 64])
        dpi = A.alloc([HG, 64])
        spr = A.alloc([HG, 64])
        spi = A.alloc([HG, 64])
        w1 = A.alloc([HG, 64])
        w2 = A.alloc([HG, 64])
        sreb = A.alloc([HG, 64], BF16)
        simb = A.alloc([HG, 64], BF16)
        cr = [A.alloc([HG]) for _ in range(4)]
        Rd, Rsp, Rw, Rsb, Rcr = P.res("dp"), P.res("sp_"), P.res("w12"), P.res("srb"), P.res("cr")
        for ci in range(nch):
            cc = slice(ci * 64, ci * 64 + 64)
            py = PB(4).rearrange("p (c t) -> p c t", t=64)
            for half in range(32 // HG):
                gs = slice(HG * half, HG * half + HG)
                for gl in range(HG):
                    gp = HG * half + gl
                    q4, cch, jj = gp % 4, gp // 4, gl // 4
                    rows = slice(32 * q4, 32 * q4 + 32) if q4 < 3 else slice(64, 128)
                    bb_ = bstb if q4 < 3 else bst3
                    P.op("pe", lambda e, rows=rows, cch=cch, cc=cc, bb_=bb_, q4=q4, jj=jj: e.matmul(
                        PB(q4)[:, jj * 64:jj * 64 + 64], lhsT=bb_[0][rows, cch, :], rhs=ub[rows, cch, cc], start=True, stop=True),
                        [Rbst[0], Ru[cch]], [Rb[q4]])
                    P.op("pe", lambda e, rows=rows, cch=cch, cc=cc, bb_=bb_, q4=q4, jj=jj: e.matmul(
                        PB(q4)[:, 128 + jj * 64:128 + jj * 64 + 64], lhsT=bb_[1][rows, cch, :], rhs=ub[rows, cch, cc], start=True,
                        stop=True), [Rbst[1], Ru[cch]], [Rb[q4]])
                rre = [Rb[0], Rb[1], Rb[2], Rb[3]]
                pre = ps_t[:, 0:4, 0:128].rearrange("p q (j t) -> p q j t", j=2)
                pim = ps_t[:, 0:4, 128:256].rearrange("p q (j t) -> p q j t", j=2)
                pv = lambda ap: ap.rearrange("p (q j) t -> p q j t", q=4)
                nv = lambda ap: ap.rearrange("p (j q) t -> p q j t", q=4)
                tc_, ts_ = nv(tcs[:, gs, :]), nv(tsn[:, gs, :])
                w1v, w2v, dprv, dpiv, sprv, spiv = [pv(x_) for x_ in (w1, w2, dpr, dpi, spr, spi)]
                dvo(lambda e, pre=pre, tc_=tc_: e.tensor_tensor(out=w1v, in0=pre, in1=tc_, op=ALU.mult), rre + [Rtab, Rw], [Rw])
                dvo(lambda e, pim=pim, ts_=ts_: e.tensor_tensor(out=w2v, in0=pim, in1=ts_, op=ALU.mult), rre + [Rtab, Rw], [Rw])
                dvo(lambda e: e.tensor_tensor(out=dpr, in0=w1, in1=w2, op=ALU.add), [Rw, Rd], [Rd])
                dvo(lambda e, pim=pim, tc_=tc_: e.tensor_tensor(out=w1v, in0=pim, in1=tc_, op=ALU.mult), rre + [Rtab, Rw], [Rw])
                dvo(lambda e, pre=pre, ts_=ts_: e.tensor_tensor(out=w2v, in0=pre, in1=ts_, op=ALU.mult), rre + [Rtab, Rw], [Rw])
                dvo(lambda e: e.tensor_tensor(out=dpi, in0=w1, in1=w2, op=ALU.subtract), [Rw, Rd], [Rd])
                nq_ = lambda ap: ap.rearrange("p (j q) -> p q j", q=4).unsqueeze(3)
                dvo(lambda e, gs=gs: e.tensor_tensor(out=cr[0], in0=rho[:, gs], in1=s5s[0][:, gs], op=ALU.mult), [Rtab, Rs5, Rcr], [Rcr])
                dvo(lambda e, gs=gs: e.tensor_tensor(out=cr[1], in0=rho[:, gs], in1=s5s[1][:, gs], op=ALU.mult), [Rtab, Rs5, Rcr], [Rcr])
                dvo(lambda e: e.tensor_tensor(out=dprv[:, :, :, 0:1], in0=dprv[:, :, :, 0:1], in1=nq_(cr[0]), op=ALU.add), [Rd, Rcr], [Rd])
                dvo(lambda e: e.tensor_tensor(out=dpiv[:, :, :, 0:1], in0=dpiv[:, :, :, 0:1], in1=nq_(cr[1]), op=ALU.add), [Rd, Rcr], [Rd])
                fl = lambda ap: ap.rearrange("p g t -> p (g t)")
                r0 = rho0[:, gs, :]
                dvo(lambda e, r0=r0: e.tensor_tensor_scan(out=fl(spr), data0=fl(r0), data1=fl(dpr), initial=0.0, op0=ALU.mult,
                                                          op1=ALU.add), [Rd, Rtab, Rsp], [Rsp])
                dvo(lambda e, r0=r0: e.tensor_tensor_scan(out=fl(spi), data0=fl(r0), data1=fl(dpi), initial=0.0, op0=ALU.mult,
                                                          op1=ALU.add), [Rd, Rtab, Rsp], [Rsp])
                dvo(lambda e, tc_=tc_: e.tensor_tensor(out=w1v, in0=sprv, in1=tc_, op=ALU.mult), [Rsp, Rtab, Rw], [Rw])
                dvo(lambda e, ts_=ts_: e.tensor_tensor(out=w2v, in0=spiv, in1=ts_, op=ALU.mult), [Rsp, Rtab, Rw], [Rw])
                dvo(lambda e: e.tensor_tensor(out=sreb, in0=w1, in1=w2, op=ALU.subtract), [Rw, Rsb], [Rsb])
                dvo(lambda e, gs=gs: e.tensor_tensor(out=nq_(s5s[0][:, gs]), in0=w1v[:, :, :, 63:64], in1=w2v[:, :, :, 63:64],
                                                     op=ALU.subtract), [Rw, Rs5], [Rs5])
                dvo(lambda e, ts_=ts_: e.tensor_tensor(out=w1v, in0=sprv, in1=ts_, op=ALU.mult), [Rsp, Rtab, Rw], [Rw])
                dvo(lambda e, tc_=tc_: e.tensor_tensor(out=w2v, in0=spiv, in1=tc_, op=ALU.mult), [Rsp, Rtab, Rw], [Rw])
                dvo(lambda e: e.tensor_tensor(out=simb, in0=w1, in1=w2, op=ALU.add), [Rw, Rsb], [Rsb])
                dvo(lambda e, gs=gs: e.tensor_tensor(out=nq_(s5s[1][:, gs]), in0=w1v[:, :, :, 63:64], in1=w2v[:, :, :, 63:64],
                                                     op=ALU.add), [Rw, Rs5], [Rs5])
                for gl in range(HG):
                    gp = HG * half + gl
                    q4, cch = gp % 4, gp // 4
                    lo = (gl % 4) * 2 + gl // 4
                    rows = slice(64 * (q4 // 2), 64 * (q4 // 2) + 64)
                    P.op("pe", lambda e, gp=gp, lo=lo, rows=rows, cch=cch, py=py, q4=q4: e.matmul(
                        py[rows, cch, :], lhsT=cqb[0][:, gp, :], rhs=sreb[:, lo, :], start=(q4 % 2 == 0), stop=False),
                        [Rcq, Rsb], [Rb[4]])
                    P.op("pe", lambda e, gp=gp, lo=lo, rows=rows, cch=cch, py=py, q4=q4: e.matmul(
                        py[rows, cch, :], lhsT=cqb[1][:, gp, :], rhs=simb[:, lo, :], start=False, stop=(q4 % 2 == 1)),
                        [Rcq, Rsb], [Rb[4]])
            sdb = ppw("sd").unsqueeze(2).to_broadcast([128, 8, 64])
            dvo(lambda e, cc=cc, sdb=sdb: e.tensor_tensor(out=ya32[:, :, cc], in0=u32[:, :, cc], in1=sdb, op=ALU.mult),
                Ru + [Rpp, Rya32], [Rya32])
            dvo(lambda e, cc=cc, py=py: e.tensor_tensor(out=ya32[:, :, cc], in0=py, in1=ya32[:, :, cc], op=ALU.add),
                [Rb[4], Rya32], [Rya32])
        ygb = ub
        Ryg = P.res("yg")
        gt = u32
        for c in range(8):
            dvo(lambda e, c=c: e.tensor_tensor(out=gt[:, c, 0:n], in0=ya32[:, c, 0:n], in1=ya32[:, c, 0:n], op=ALU.mult),
                [Rya32] + Ru, [Ru[c]])
            dvo(lambda e, c=c: e.tensor_scalar(out=gt[:, c, 0:n], in0=gt[:, c, 0:n], scalar1=0.044715, scalar2=1.0,
                                               op0=ALU.mult, op1=ALU.add), [Ru[c]], [Ru[c]])
            dvo(lambda e, c=c: e.tensor_tensor(out=gt[:, c, 0:n], in0=gt[:, c, 0:n], in1=ya32[:, c, 0:n], op=ALU.mult),
                [Ru[c], Rya32], [Ru[c]])
            aco(lambda e, c=c: e.activation(out=gt[:, c, 0:n], in_=gt[:, c, 0:n], func=AF.Sigmoid, scale=GELU_K), [Ru[c]], [Ru[c]])
            dvo(lambda e, c=c: e.tensor_tensor(out=ya32[:, c, 0:n], in0=ya32[:, c, 0:n], in1=gt[:, c, 0:n], op=ALU.mult),
                [Ru[c], Rya32], [Rya32])
            aco(lambda e, c=c: e.activation(out=ygb[:, c, 0:n], in_=ya32[:, c, 0:n], func=AF.Copy), [Rya32, Ryg], [Ryg])
        P.barrier()
        wg_b = [x_.rearrange("p a b -> p (a b)").bitcast(BF16).rearrange("p (a b) -> p a b", a=8) for x_ in (dpr, dpi)]
        Rwgl = [P.pres("wglu%d" % i_) for i_ in range(2)]
        wgluv = wglu.rearrange("(k p) c -> p k c", p=128)
        for c in range(8):
            b = c % 2
            P.dma("pool", lambda e, b=b, c=c: e.dma_start(out=wg_b[b], in_=wgluv[:, :, 128 * c:128 * c + 128]), Rwgl[b])
            for k in range(8):
                P.op("pe", lambda e, b=b, k=k: e.matmul(PB(5 + b)[:, 0:n], lhsT=wg_b[b][:, k, :], rhs=ygb[:, k, 0:n],
                                                        start=(k == 0), stop=(k == 7)), [Rwgl[b], Ryg], [Rb[5 + b]])
            aco(lambda e, c=c, b=b: e.activation(out=gt[:, c, 0:n], in_=PB(5 + b)[:, 0:n], func=AF.Sigmoid), [Rb[5 + b], Ru[c]], [Ru[c]])
            dvo(lambda e, c=c: e.tensor_tensor(out=yaT[:, c, 0:n], in0=ya32[:, c, 0:n], in1=gt[:, c, 0:n], op=ALU.mult),
                [Rya32, Ru[c]], [Rya[c]])
        P.barrier()
        A.release(ms)

        if g["dbg"] is not None:
            dd = g["dbg"]
            Rdbg = g["Rdbg"]
            mdb = A.mark()
            dbt = A.alloc([16, n])
            Rdbt = P.res("dbt")
            for c in range(8):
                P.op("dve", lambda e, c=c: e.tensor_copy(out=dbt[:, c, 0:n], in_=yaT[:, c, 0:n]), [Rya[c], Rdbt], [Rdbt])
                P.op("dve", lambda e, c=c: e.tensor_copy(out=dbt[:, 8 + c, 0:n], in_=ybT[:, c, 0:n]), [Ryb[c], Rdbt], [Rdbt])
            P.dma("sp", lambda e: e.dma_start(out=dd[:, :, t0:t0 + n].rearrange("j p t -> p j t"), in_=dbt[:, :, 0:n]), Rdbg,
                  reads=[Rdbt])
            P.barrier()
            A.release(mdb)
        mT = A.alloc([NDC, TILE], BF16)
        RmT = P.res("mT")
        wu_b = [A.alloc([8, 128], BF16) for _ in range(4)]
        Rwu_ = [P.pres("wup%d" % i_) for i_ in range(4)]
        gaT = [A.alloc([TILE]) for _ in range(2)]
        Rga = [P.res("ga") for _ in range(2)]
        wupsv = wups.rearrange("(k p) c -> p k c", p=128)
        wuprv = wupr.rearrange("(k p) c -> p k c", p=128)
        for d in range(NDC):
            b = d % 2
            dc = slice(128 * d, 128 * d + 128)
            P.dma("pool", lambda e, b=b, dc=dc: e.dma_start(out=wu_b[b], in_=wupsv[:, :, dc]), Rwu_[b])
            P.dma("pool", lambda e, b=b, dc=dc: e.dma_start(out=wu_b[2 + b], in_=wuprv[:, :, dc]), Rwu_[2 + b])
            proj(COL_GA + 128 * d, 128, n, PB(0), Rb[0])
            aco(lambda e: e.activation(out=gaT[0][:, 0:n], in_=PB(0)[:, 0:n], func=AF.Sigmoid), [Rb[0]], [Rga[0]])
            proj(COL_GB + 128 * d, 128, n, PB(1), Rb[1])
            aco(lambda e: e.activation(out=gaT[1][:, 0:n], in_=PB(1)[:, 0:n], func=AF.Sigmoid), [Rb[1]], [Rga[1]])
            for k in range(8):
                P.op("pe", lambda e, b=b, k=k: e.matmul(PB(2)[:, 0:n], lhsT=wu_b[b][:, k, :], rhs=yaT[:, k, 0:n],
                                                        start=(k == 0), stop=(k == 7)), [Rwu_[b], Rya[k]], [Rb[2]])
            for k in range(8):
                P.op("pe", lambda e, b=b, k=k: e.matmul(PB(3)[:, 0:n], lhsT=wu_b[2 + b][:, k, :], rhs=ybT[:, k, 0:n],
                                                        start=(k == 0), stop=(k == 7)), [Rwu_[2 + b], Ryb[k]], [Rb[3]])
            dvo(lambda e: e.tensor_tensor(out=gaT[0][:, 0:n], in0=PB(2)[:, 0:n], in1=gaT[0][:, 0:n], op=ALU.mult),
                [Rb[2], Rga[0]], [Rga[0]])
            dvo(lambda e: e.tensor_tensor(out=gaT[1][:, 0:n], in0=PB(3)[:, 0:n], in1=gaT[1][:, 0:n], op=ALU.mult),
                [Rb[3], Rga[1]], [Rga[1]])
            dvo(lambda e, d=d: e.tensor_tensor(out=mT[:, d, 0:n], in0=gaT[0][:, 0:n], in1=gaT[1][:, 0:n], op=ALU.add),
                [Rga[0], Rga[1]], [RmT])
        woutv = wout.rearrange("(j p) c -> p j c", p=128)
        for d in range(NDC):
            b = wbi[0] % 3
            wbi[0] += 1
            dc = slice(128 * d, 128 * d + 128)
            bi = 4 + d % 2
            wb_ = wbuf[b]
            P.dma("pool", lambda e, wb_=wb_, dc=dc: e.dma_start(out=wb_, in_=woutv[:, :, dc]), Rwb[b])
            for j in range(NDC):
                P.op("pe", lambda e, wb_=wb_, j=j, bi=bi: e.matmul(PB(bi)[:, 0:n], lhsT=wb_[:, j, :], rhs=mT[:, j, 0:n],
                                                                   start=(j == 0), stop=(j == NDC - 1)), [Rwb[b], RmT], [Rb[bi]])
            dvo(lambda e, d=d, bi=bi: e.tensor_tensor(out=X[:, d, 0:n], in0=PB(bi)[:, 0:n], in1=X[:, d, 0:n], op=ALU.add),
                [Rb[bi], RX[d]], [RX[d]])
        P.barrier()
        A.release(mk)

    return mixer


def make_consts():
    c = np.zeros((128, NCST), np.float32)
    o = CS["ones"][0]
    c[:, o:o + 128] = 1.0
    o = CS["bones"][0]
    c[0:64, o:o + 64] = 1.0
    c[64:128, o + 64:o + 128] = 1.0
    o = CS["ident"][0]
    c[:, o:o + 128] = np.eye(128, dtype=np.float32)
    i = np.arange(64)[:, None]
    t = np.arange(64)[None, :]
    su = (i < t).astype(np.float32)
    iu = (i <= t).astype(np.float32)
    m1 = np.zeros((64, 2, 2, 2, 64), np.float32)
    m1[:, :, :, 0, :] = su[:, None, None, :]
    m1[:, :, :, 1, :] = iu[:, None, None, :]
    o = CS["m1"][0]
    c[0:64, o:o + 512] = m1.reshape(64, 512)
    sl = (i > t).astype(np.float32)
    o = CS["msl"][0]
    c[0:64, o:o + 128] = np.concatenate([sl, sl], 1)
    o = CS["ifree"][0]
    c[0:64, o:o + 128] = np.concatenate([np.eye(64), np.eye(64)], 1)
    o = CS["rmask"][0]
    rm = np.ones(512, np.float32)
    rm[0::64] = 0.0
    c[:, o:o + 512] = rm[None, :]
    return c


def pcol(v):
    v = np.asarray(v, np.float32)
    return np.ascontiguousarray(v.reshape(-1, 128).T)


def layer_inputs(inp, i, consts):
    f32 = np.float32
    p = np.zeros((128, NPP), f32)

    def put(name, arr):
        o, w = PP[name]
        arr = np.asarray(arr, f32)
        p[0:arr.shape[0], o:o + arr.shape[1]] = arr

    put("g1", pcol(inp["ffn1_norm"][i]))
    put("gm", pcol(inp["mix_norm"][i]))
    put("g2", pcol(inp["ffn2_norm"][i]))
    put("gf", pcol(inp["final_norm"]))
    mu = np.asarray(inp["mu_shift"][i], f32)
    put("mur", pcol(mu[0:1024]))
    put("muk", pcol(mu[1024:2048]))
    put("muv", pcol(mu[2048:3072]))
    put("muwa", mu[3072:3200].reshape(128, 1))
    put("mug1", mu[3200:3328].reshape(128, 1))
    put("mug2", mu[3328:3360].reshape(32, 1))
    if i > 0:
        put("muvr", np.asarray(inp["mu_vres"][i - 1], f32).reshape(32, 1))
        put("v0", pcol(inp["rwkv_v0"][i - 1]))
        put("flag", np.ones((128, 1), f32))
    put("w0", pcol(inp["rwkv_w0"][i]))
    put("a0", pcol(inp["rwkv_a0"][i]))
    put("kk", pcol(inp["rwkv_k_k"][i]))
    put("ka", pcol(inp["rwkv_k_a"][i]))
    put("rk", pcol(np.asarray(inp["rwkv_r_k"][i]).reshape(-1)))
    put("lw", pcol(inp["rwkv_lnx_w"][i]))
    put("lb", pcol(inp["rwkv_lnx_b"][i]))
    put("sd", pcol(inp["ssm_d"][i]))
    put("lre", pcol(np.asarray(inp["ssm_lambda_re"][i]).reshape(-1)))
    put("lim", pcol(np.asarray(inp["ssm_lambda_im"][i]).reshape(-1)))
    ldt = np.repeat(np.asarray(inp["ssm_log_dt"][i], f32), 64)
    put("ldt", pcol(ldt))

    if i == 0:
        w_in = np.zeros((D, PIN), f32)
        w_in[:, :8480] = inp["w_in_first"]
        v2 = np.zeros((32, 1024), f32)
    else:
        w_in = np.ascontiguousarray(inp["w_in_rest"][i - 1], dtype=f32)
        v2 = np.ascontiguousarray(inp["rwkv_v2"][i - 1], dtype=f32)
    wa2 = np.concatenate([inp["rwkv_w2"][i], inp["rwkv_a2"][i]], 0).astype(f32)
    g2 = np.asarray(inp["rwkv_g2"][i], f32)
    def bst(b):
        b = np.asarray(b, f32)
        out = np.zeros((4, 2, 16, 8, 2, 64), f32)
        bb = b.reshape(8, 4, 2, 64, 16)
        for g2_ in range(2):
            out[:, g2_, :, :, g2_, :] = bb[:, :, g2_, :, :].transpose(1, 3, 0, 2)
        return np.ascontiguousarray(out.reshape(128, 8 * 128))

    def bst3(b):
        o = bst(b).copy()
        o[0:96] = 0.0
        return o

    def cst_(cm):
        cm = np.asarray(cm, f32)
        out = np.zeros((2, 64, 32, 2, 2, 16), f32)
        cc = cm.reshape(32, 2, 16, 64)
        for g2_ in range(2):
            for hf in range(2):
                out[g2_, :, hf::2, hf, g2_, :] = cc[hf::2, g2_, :, :].transpose(2, 0, 1)
        return np.ascontiguousarray(out.reshape(128, 32 * 64))

    return {
        "pp": p, "cst": consts,
        "wg1": np.ascontiguousarray(inp["ffn1_w_gate"][i], dtype=f32),
        "wu1": np.ascontiguousarray(inp["ffn1_w_up"][i], dtype=f32),
        "wd1": np.ascontiguousarray(inp["ffn1_w_down"][i], dtype=f32),
        "wg2": np.ascontiguousarray(inp["ffn2_w_gate"][i], dtype=f32),
        "wu2": np.ascontiguousarray(inp["ffn2_w_up"][i], dtype=f32),
        "wd2": np.ascontiguousarray(inp["ffn2_w_down"][i], dtype=f32),
        "win": w_in,
        "wglu": np.ascontiguousarray(inp["ssm_w_glu"][i], dtype=f32),
        "wups": np.ascontiguousarray(inp["w_up_ssm"][i], dtype=f32),
        "wupr": np.ascontiguousarray(inp["w_up_rwkv"][i], dtype=f32),
        "wout": np.ascontiguousarray(inp["w_out"][i], dtype=f32),
        "wa2": wa2, "g2a": np.ascontiguousarray(g2[0:128]), "g2b": np.ascontiguousarray(g2[128:160]), "v2": v2,
        "bre": bst(inp["ssm_b_re"][i]), "bim": bst(inp["ssm_b_im"][i]),
        "bre3": bst3(inp["ssm_b_re"][i]), "bim3": bst3(inp["ssm_b_im"][i]),
        "cre": cst_(inp["ssm_c_re"][i]), "cim": cst_(inp["ssm_c_im"][i]),
    }


def to_fm(a, nchunks):
    T = a.shape[0]
    return np.ascontiguousarray(a.reshape(T, nchunks, 128).transpose(1, 2, 0))


_PROG = {}


def kernel(**inp):
    inp = {k: np.asarray(v) for k, v in inp.items()}
    x = inp["x"].astype(np.float32)
    B = x.shape[0]
    depth = inp["ffn1_norm"].shape[0]
    if "nc" not in _PROG:
        _PROG["nc"] = build_program()
    nc = _PROG["nc"]
    consts = make_consts()
    meta = inp["meta_tokens"].astype(np.float32)
    xs = []
    for b in range(B):
        h = np.zeros((TPAD, D), np.float32)
        h[0:NMETA] = meta
        h[NMETA:NMETA + SEQ] = x[b]
        xs.append(to_fm(h, NDC))
    vfs = [np.zeros((8, 128, TPAD), np.float32) for _ in range(B)]
    z = None
    for i in range(depth):
        li = layer_inputs(inp, i, consts)
        in_maps = []
        for b in range(B):
            m = dict(li)
            m["xT"] = xs[b]
            m["vfT"] = vfs[b]
            in_maps.append(m)
        res = run_bass_kernel_spmd(nc, in_maps, core_ids=list(range(B)))
        xs = [r["yT"] for r in res.results]
        if i == 0:
            vfs = [r["vfo"] for r in res.results]
        z = [r["zT"] for r in res.results]
    out = np.stack([zb.transpose(2, 0, 1).reshape(TPAD, D)[NMETA:NMETA + SEQ] for zb in z], 0)
    return out.astype(np.float32)
```
